# Optimizing a Trainium2 kernel written in Bass

```python
import math
import jax, jax.numpy as jnp
from jax import lax
import numpy as np

D_MODEL = 1024
BATCH = 8
SEQ = 2048
DEPTH = 1

CHUNK = 64
D_MIX = D_MODEL
D_LRU = D_MIX // 2
D_CONV = D_MIX - D_LRU
LRU_HEADS = 8
LRU_HEAD_DIM = D_LRU // LRU_HEADS
CONV_GROUPS = 8
CONV_GROUP_DIM = D_CONV // CONV_GROUPS
LRU_CONV_WIDTH = 4
SHORT_CONV_WIDTH = 3
D_FF = 4 * D_MODEL
C_GATE = 8.0
MIN_RAD = 0.9
MAX_RAD = 0.999
EPS = 1e-6
N_ADA = 6
D_IN = 2 * D_LRU + 3 * D_CONV

kernel_name = "hybrid_rglru_shortconv_adaln_block"


def rmsnorm(x, g):
    xf = x.astype(jnp.float32)
    y = xf * lax.rsqrt(jnp.mean(xf * xf, axis=-1, keepdims=True) + EPS)
    return (y * g.astype(jnp.float32)).astype(x.dtype)


def headwise_rmsnorm(y, g, n_heads):
    b, s, w = y.shape
    yh = y.reshape(b, s, n_heads, w // n_heads).astype(jnp.float32)
    yh = yh * lax.rsqrt(jnp.mean(yh * yh, axis=-1, keepdims=True) + EPS)
    return (yh.reshape(b, s, w) * g.astype(jnp.float32)).astype(y.dtype)


def causal_depthwise_conv(x, w):
    k, ch = w.shape
    rhs = w[:, None, :].astype(x.dtype)
    return lax.conv_general_dilated(
        x, rhs, window_strides=(1,), padding=[(k - 1, 0)],
        dimension_numbers=("NWC", "WIO", "NWC"), feature_group_count=ch)


def chunked_linear_scan(a, b):
    bn, s, w = a.shape
    nc = s // CHUNK
    a = a.reshape(bn, nc, CHUNK, w)
    b = b.reshape(bn, nc, CHUNK, w)

    def combine(left, right):
        al, bl = left
        ar, br = right
        return al * ar, ar * bl + br

    a_cum, h_loc = lax.associative_scan(combine, (a, b), axis=2)

    def step(h, inp):
        a_last, h_last = inp
        return a_last * h + h_last, h

    _, h_in = lax.scan(step, jnp.zeros((bn, w), jnp.float32),
                       (jnp.swapaxes(a_cum[:, :, -1], 0, 1), jnp.swapaxes(h_loc[:, :, -1], 0, 1)))
    h_in = jnp.swapaxes(h_in, 0, 1)
    h = h_loc + a_cum * h_in[:, :, None, :]
    return h.reshape(bn, s, w)


def rg_lru(xl, gate_a_w, gate_a_b, gate_x_w, gate_x_b, a_param):
    bn, s, w = xl.shape
    xh = xl.reshape(bn, s, LRU_HEADS, LRU_HEAD_DIM)
    r = jax.nn.sigmoid(jnp.einsum("bshi,hij->bshj", xh, gate_a_w).reshape(bn, s, w) + gate_a_b)
    i = jax.nn.sigmoid(jnp.einsum("bshi,hij->bshj", xh, gate_x_w).reshape(bn, s, w) + gate_x_b)
    log_a = -C_GATE * r.astype(jnp.float32) * jax.nn.softplus(a_param.astype(jnp.float32))
    a = jnp.exp(log_a)
    mult = jnp.sqrt(-jnp.expm1(2.0 * log_a))
    is_first = (jnp.arange(s) == 0)[None, :, None]
    mult = jnp.where(is_first, jnp.ones_like(mult), mult)
    bx = mult * (i * xl).astype(jnp.float32)
    return chunked_linear_scan(a, bx).astype(xl.dtype)


def setup_inputs(seed: int = 0) -> dict:
    key = jax.random.key(seed)
    ks = jax.random.split(key, 24)
    f32 = jnp.float32
    nrm = lambda k, shape, scale: (jax.random.normal(k, shape, f32) * scale)
    x = jax.random.normal(ks[0], (BATCH, SEQ, D_MODEL), f32)
    c = jax.random.normal(ks[1], (BATCH, D_MODEL), f32)
    ada_w = nrm(ks[2], (DEPTH, D_MODEL, N_ADA * D_MODEL), 0.5 * D_MODEL ** -0.5)
    ada_b = nrm(ks[3], (DEPTH, N_ADA * D_MODEL), 0.01)
    norm1_g = 1.0 + nrm(ks[4], (DEPTH, D_MODEL), 0.02)
    w_in = nrm(ks[5], (DEPTH, D_MODEL, D_IN), D_MODEL ** -0.5)
    lru_conv_w = nrm(ks[6], (DEPTH, LRU_CONV_WIDTH, D_LRU), LRU_CONV_WIDTH ** -0.5)
    lru_conv_b = nrm(ks[7], (DEPTH, D_LRU), 0.01)
    gate_a_w = nrm(ks[8], (DEPTH, LRU_HEADS, LRU_HEAD_DIM, LRU_HEAD_DIM), LRU_HEAD_DIM ** -0.5)
    gate_a_b = nrm(ks[9], (DEPTH, D_LRU), 0.01)
    gate_x_w = nrm(ks[10], (DEPTH, LRU_HEADS, LRU_HEAD_DIM, LRU_HEAD_DIM), LRU_HEAD_DIM ** -0.5)
    gate_x_b = nrm(ks[11], (DEPTH, D_LRU), 0.01)
    u = jax.random.uniform(ks[12], (DEPTH, D_LRU), f32, MIN_RAD ** 2, MAX_RAD ** 2)
    a_param = jnp.log(jnp.expm1(-0.5 * jnp.log(u)))
    short_conv_w = nrm(ks[13], (DEPTH, SHORT_CONV_WIDTH, D_CONV), SHORT_CONV_WIDTH ** -0.5)
    lru_out_g = 1.0 + nrm(ks[14], (DEPTH, D_LRU), 0.02)
    conv_out_g = 1.0 + nrm(ks[15], (DEPTH, D_CONV), 0.02)
    w_out = nrm(ks[16], (DEPTH, D_MIX, D_MODEL), D_MIX ** -0.5)
    norm2_g = 1.0 + nrm(ks[17], (DEPTH, D_MODEL), 0.02)
    w_mlp1 = nrm(ks[18], (DEPTH, D_MODEL, D_FF), D_MODEL ** -0.5)
    w_mlp2 = nrm(ks[19], (DEPTH, D_FF, D_MODEL), D_FF ** -0.5)
    final_g = 1.0 + nrm(ks[20], (D_MODEL,), 0.02)
    return {"x": x, "c": c, "ada_w": ada_w, "ada_b": ada_b, "norm1_g": norm1_g,
            "w_in": w_in, "lru_conv_w": lru_conv_w, "lru_conv_b": lru_conv_b,
            "gate_a_w": gate_a_w, "gate_a_b": gate_a_b, "gate_x_w": gate_x_w,
            "gate_x_b": gate_x_b, "a_param": a_param, "short_conv_w": short_conv_w,
            "lru_out_g": lru_out_g, "conv_out_g": conv_out_g, "w_out": w_out,
            "norm2_g": norm2_g, "w_mlp1": w_mlp1, "w_mlp2": w_mlp2, "final_g": final_g}


def reference(x, c, ada_w, ada_b, norm1_g, w_in, lru_conv_w, lru_conv_b, gate_a_w, gate_a_b,
              gate_x_w, gate_x_b, a_param, short_conv_w, lru_out_g, conv_out_g, w_out,
              norm2_g, w_mlp1, w_mlp2, final_g):
    sc = jax.nn.silu(c)
    for l in range(DEPTH):
        mod = sc @ ada_w[l] + ada_b[l]
        shift1, scale1, gate1, shift2, scale2, gate2 = jnp.split(mod[:, None, :], N_ADA, axis=-1)

        h = rmsnorm(x, norm1_g[l]) * (1.0 + scale1) + shift1
        proj = h @ w_in[l]
        u_lx, u_ly, u_b, u_c, u_v = jnp.split(
            proj, np.cumsum([D_LRU, D_LRU, D_CONV, D_CONV])[:].tolist(), axis=-1)

        xl = causal_depthwise_conv(u_lx, lru_conv_w[l]) + lru_conv_b[l]
        hl = rg_lru(xl, gate_a_w[l], gate_a_b[l], gate_x_w[l], gate_x_b[l], a_param[l])
        y_lru = headwise_rmsnorm(jax.nn.gelu(u_ly) * hl, lru_out_g[l], LRU_HEADS)

        y_conv = u_b * causal_depthwise_conv(u_c * u_v, short_conv_w[l])
        y_conv = headwise_rmsnorm(y_conv, conv_out_g[l], CONV_GROUPS)

        mixed = jnp.concatenate([y_lru, y_conv], axis=-1) @ w_out[l]
        x = x + gate1 * mixed

        h2 = rmsnorm(x, norm2_g[l]) * (1.0 + scale2) + shift2
        x = x + gate2 * (jnp.square(jax.nn.relu(h2 @ w_mlp1[l])) @ w_mlp2[l])
    return rmsnorm(x, final_g)
```

```python
import heapq
from contextlib import ExitStack

import concourse.bass as bass
import concourse.mybir as mybir

F32 = mybir.dt.float32
BF16 = mybir.dt.bfloat16
ALU = mybir.AluOpType
AF = mybir.ActivationFunctionType

ENGS = ("pe", "act", "dve", "pool", "sp")
CP_ALPHA = 0.8


class Tile:
    __slots__ = ("ap", "name", "last_w", "readers", "excl")

    def __init__(self, ap, name, excl=False):
        self.ap = ap
        self.name = name
        self.last_w = None
        self.readers = []
        self.excl = excl

    def __getitem__(self, key):
        return self.ap[key]


class Op:
    __slots__ = ("id", "eng", "fn", "deps", "cost", "dma_key", "users", "name",
                 "start", "finish", "pos", "signal", "semval", "nbytes", "group")

    def __init__(self):
        self.users = []
        self.signal = False
        self.semval = 0
        self.dma_key = None
        self.group = None


class Sched:
    def __init__(self, nc):
        self.nc = nc
        self.ops = []
        self.stack = ExitStack()
        self.n_sb = 0
        self.dma_count = {}
        import os
        self.cut = int(os.environ["KCUT"]) if os.environ.get("KCUT") else None

    def sbuf(self, name, shape, dtype):
        h = self.stack.enter_context(self.nc.sbuf_tensor(name, list(shape), dtype))
        return h

    def psum(self, name, shape, dtype):
        h = self.stack.enter_context(self.nc.psum_tensor(name, list(shape), dtype))
        return h

    def tile(self, name, shape, dtype):
        h = self.sbuf(name, shape, dtype)
        return Tile(h[tuple(slice(None) for _ in shape)], name)

    def view(self, ap, name):
        return Tile(ap, name)

    def add(self, eng, fn, reads=(), writes=(), cost=100.0, name="", dma_key=None, nbytes=0, after=()):
        o = Op()
        if self.cut is not None and len(self.ops) >= self.cut and not name.startswith("dbg"):
            o.id = -1
            return o
        o.id = len(self.ops)
        o.eng = eng
        o.fn = fn
        o.cost = float(cost)
        o.name = name
        o.nbytes = nbytes
        deps = set()
        xr = [t for t in reads if t.excl and t not in writes]
        if xr:
            reads = [t for t in reads if not t.excl]
            writes = list(writes) + xr
        for t in reads:
            if t.last_w is not None:
                deps.add(t.last_w)
        for t in writes:
            if t.last_w is not None:
                deps.add(t.last_w)
            deps.update(t.readers)
        deps.update(i for i in after if i is not None and i >= 0)
        deps.discard(o.id)
        o.deps = deps
        if dma_key is not None:
            o.dma_key = dma_key
        for t in reads:
            t.readers.append(o.id)
        for t in writes:
            t.last_w = o.id
            t.readers = []
        self.ops.append(o)
        return o

    def dma(self, out_ap, in_ap, reads=(), writes=(), key=None, nbytes=0, name="", eng="sp", after=(), **kw):
        assert key is not None
        if eng == "pool":
            key = key + "_sw"
        return self.add(eng, lambda e: e.dma_start(out=out_ap, in_=in_ap, **kw), reads, writes,
                        cost=(1100.0 if eng == "pool" else 60.0), name=name or key, dma_key=key, nbytes=nbytes, after=after)

    def simulate(self, dma_bw=450.0, dma_lat=1500.0, sem_lat=100.0, verbose=True):
        ops = self.ops
        n = len(ops)
        for o in ops:
            o.users = []
        for o in ops:
            for d in o.deps:
                ops[d].users.append(o.id)
        remaining = [len(o.deps) for o in ops]
        prio = list(range(n))
        if CP_ALPHA > 0.0:
            cp = [0.0] * n
            for o in reversed(ops):
                best = 0.0
                for u in o.users:
                    if cp[u] > best:
                        best = cp[u]
                cp[o.id] = best + o.cost + (o.nbytes / dma_bw + dma_lat if o.dma_key is not None else 0.0)
            total = max(cp) if n else 1.0
            for o in ops:
                prio[o.id] = (o.id / n) * total * (1.0 - CP_ALPHA) - cp[o.id] * CP_ALPHA
        ready_time = [0.0] * n
        future = {e: [] for e in ENGS}
        avail = {e: [] for e in ENGS}
        eng_free = {e: 0.0 for e in ENGS}
        busy = {e: 0.0 for e in ENGS}
        order = {e: [] for e in ENGS}
        dma_free = 0.0
        act_table = [None]
        for o in ops:
            if remaining[o.id] == 0:
                heapq.heappush(future[o.eng], (0.0, o.id))
        t = 0.0
        nsched = 0
        while nsched < n:
            progressed = False
            for e in ENGS:
                fu = future[e]
                av = avail[e]
                while fu and fu[0][0] <= t:
                    _, i = heapq.heappop(fu)
                    heapq.heappush(av, (prio[i], i))
                if eng_free[e] <= t and av:
                    _, i = heapq.heappop(av)
                    o = ops[i]
                    o.start = t
                    if e == "act" and o.group is not None and o.group != act_table[0]:
                        act_table[0] = o.group
                        o.cost += 1280.0
                    if o.dma_key is not None:
                        ts = max(t + o.cost, dma_free)
                        te = ts + o.nbytes / dma_bw
                        dma_free = te
                        o.finish = te + dma_lat
                        eng_free[e] = t + o.cost
                        busy[e] += o.cost
                    else:
                        o.finish = t + o.cost
                        eng_free[e] = o.finish
                        busy[e] += o.cost
                    order[e].append(i)
                    nsched += 1
                    progressed = True
                    for u in o.users:
                        remaining[u] -= 1
                        rt = o.finish + (sem_lat if ops[u].eng != e or o.dma_key is not None else 0.0)
                        if rt > ready_time[u]:
                            ready_time[u] = rt
                        if remaining[u] == 0:
                            heapq.heappush(future[ops[u].eng], (ready_time[u], u))
            if not progressed:
                nxt = float("inf")
                for e in ENGS:
                    if future[e] or avail[e]:
                        cand = eng_free[e] if avail[e] else float("inf")
                        if future[e]:
                            cand = min(cand, max(future[e][0][0], eng_free[e]))
                        if cand > t:
                            nxt = min(nxt, cand)
                        else:
                            nxt = min(nxt, t + 1.0)
                assert nxt < float("inf"), "scheduler stuck"
                t = nxt
        self.order = order
        makespan = max(o.finish for o in ops)
        self.makespan = makespan
        if verbose:
            print(f"[sched] ops={n} makespan={makespan/1000:.1f}us " +
                  " ".join(f"{e}:{busy[e]/1000:.0f}us/{len(order[e])}" for e in ENGS), flush=True)
        return makespan

    def emit(self, final_waits=()):
        nc = self.nc
        ops = self.ops
        order = self.order
        for e in ENGS:
            for p, i in enumerate(order[e]):
                ops[i].pos = p
        for o in ops:
            o.signal = False
        for o in ops:
            for d in o.deps:
                do = ops[d]
                if do.dma_key is not None:
                    continue
                if do.eng != o.eng:
                    do.signal = True
                elif o.eng != "pe":
                    do.signal = True
        for i in final_waits:
            if ops[i].dma_key is None:
                ops[i].signal = True
        for e in ENGS:
            c = 0
            for i in order[e]:
                o = ops[i]
                if o.dma_key is None and o.signal:
                    c += 1
                    o.semval = c
        dcount = {}
        dma_keys = []
        for o in ops:
            if o.dma_key is not None:
                if o.dma_key not in dcount:
                    dcount[o.dma_key] = 0
                    dma_keys.append(o.dma_key)
                dcount[o.dma_key] += 16
                o.semval = dcount[o.dma_key]
        sems = {}
        for e in ENGS:
            sems[e] = self.stack.enter_context(nc.semaphore("s_" + e))
        for k in dma_keys:
            sems["dma:" + k] = self.stack.enter_context(nc.semaphore("d_" + k))
        handles = {"pe": nc.tensor, "act": nc.scalar, "dve": nc.vector, "pool": nc.gpsimd, "sp": nc.sync}
        nwaits = 0

        def emit_engine(e, eh):
            nonlocal nwaits
            waited = {}
            for i in order[e]:
                o = ops[i]
                need = {}
                for d in o.deps:
                    do = ops[d]
                    if do.dma_key is not None:
                        k = "dma:" + do.dma_key
                        v = do.semval
                    else:
                        if do.eng == e and e == "pe":
                            continue
                        k = do.eng
                        v = do.semval
                    if v > need.get(k, 0):
                        need[k] = v
                for k, v in need.items():
                    if waited.get(k, 0) >= v:
                        continue
                    eh.wait_ge(sems[k], v)
                    waited[k] = v
                    nwaits += 1
                ins = o.fn(eh)
                if o.dma_key is not None:
                    ins.then_inc(sems["dma:" + o.dma_key], 16)
                elif o.signal:
                    ins.then_inc(sems[e], 1)
            if e == "sp":
                for i in final_waits:
                    o = ops[i]
                    k = ("dma:" + o.dma_key) if o.dma_key is not None else o.eng
                    if waited.get(k, 0) < o.semval:
                        eh.wait_ge(sems[k], o.semval)
                        waited[k] = o.semval

        with nc.Block() as block:
            @block.sync
            def _(eh):
                emit_engine("sp", eh)

            @block.scalar
            def _(eh):
                emit_engine("act", eh)

            @block.vector
            def _(eh):
                emit_engine("dve", eh)

            @block.gpsimd
            def _(eh):
                emit_engine("pool", eh)

            @block.tensor
            def _(eh):
                emit_engine("pe", eh)
        print(f"[emit] waits={nwaits} sems={len(sems)}", flush=True)

    def close(self):
        self.stack.close()


def c_mm(n, fp32=False):
    return max(64, n) / 2.12 * (4 if fp32 else 1) + 2


def c_act(n, psum=False):
    return ((172 if psum else 224) + n) / 1.2


def c_dve(n, mode=1, psum=False):
    if psum:
        return (120 + n) / 0.96
    return (150 + n / mode) / 0.96


def c_pool(n):
    return (200 + 2 * n) / 0.96


import numpy as np
from concourse.bass_utils import run_bass_kernel_spmd

SEQ = 2048
D = 1024
DIN = 2560
DFF = 4096
NT = SEQ // 128
NB = SEQ // 512
EPS = 1e-6
GELU_C = 0.7978845608028654
NS = 132

C_C = 0
C_ADAB = 8
C_N1G = 56
C_N2G = 64
C_LCW = 72
C_LCB = 88
C_GAB = 92
C_GXB = 96
C_AP = 100
C_LOG = 104
C_COG = 108
C_SCW = 112
C_FG = 124


class RR:
    def __init__(self, tiles):
        self.tiles = tiles
        self.i = 0

    def next(self):
        t = self.tiles[self.i % len(self.tiles)]
        self.i += 1
        return t


def build_program(verbose=True):
    nc = bass.Bass("TRN2", target_bir_lowering=False)
    S = Sched(nc)
    add = S.add

    def din(name, shape):
        return nc.dram_tensor(name, list(shape), F32, kind="ExternalInput").ap()

    x_d = din("x", [SEQ, D])
    small_d = din("small", [128, NS])
    adaw_d = din("ada_w", [D, 6 * D])
    win_d = din("w_in", [D, DIN])
    wout_d = din("w_out", [D, D])
    w1_d = din("w_mlp1", [D, DFF])
    w2_d = din("w_mlp2", [DFF, D])
    gaw_d = din("gaw", [512, 64])
    gxw_d = din("gxw", [512, 64])
    y_d = nc.dram_tensor("y", [SEQ, D], F32, kind="ExternalOutput").ap()

    ARENA_BYTES = 212480
    arena = S.sbuf("arena", [128, ARENA_BYTES // 4], F32)
    cur = [0]

    def carve_at(off, shape, dtype, name):
        n = 1
        for s in shape:
            n *= s
        nbytes = n * (4 if dtype == F32 else 2)
        assert off % 4 == 0 and nbytes % 4 == 0
        assert off + nbytes <= ARENA_BYTES, (name, off, nbytes)
        ap = arena[:, off // 4:(off + nbytes) // 4]
        if dtype != F32:
            ap = ap.bitcast(dtype)
        if len(shape) == 2:
            ap = ap.rearrange("p (a b) -> p a b", a=shape[0])
        elif len(shape) == 3:
            ap = ap.rearrange("p (a b c) -> p a b c", a=shape[0], b=shape[1])
        return ap, nbytes

    def carve(shape, dtype, name):
        ap, nbytes = carve_at(cur[0], shape, dtype, name)
        cur[0] += nbytes
        return ap

    def T(shape, dtype, name):
        return Tile(carve(shape, dtype, name), name)

    xbuf = [T([1024], F32, f"x{i}") for i in range(8)]
    xn = [T([1024], F32, f"xn{i}") for i in range(2)]
    stg_off = [cur[0], cur[0] + 8192]
    stg = [T([2048], F32, f"stg{i}") for i in range(2)]
    stg_bf = [carve_at(stg_off[i], [8, 512], BF16, "stgbf")[0] for i in range(2)]
    stg_f3 = [carve_at(stg_off[i], [2, 1024], F32, "stgf3")[0] for i in range(2)]
    wout_full = carve([8, 1024], BF16, "wout")
    Wout = [Tile(wout_full[:, k, :], f"wout{k}") for k in range(8)]
    gate_bc = T([1024], F32, "gate_bc")
    ident = T([128], F32, "ident")
    ones = T([128], F32, "ones")
    Wa = [T([128], BF16, f"wa{c}") for c in range(4)]
    Wx = [T([128], BF16, f"wx{c}") for c in range(4)]
    small = T([NS], F32, "small")
    mc = [T([8], F32, f"mc{i}") for i in range(6)]
    gs1 = T([8], F32, "gs1")
    gs2 = T([8], F32, "gs2")
    sc_th = T([8], F32, "sc_th")
    sc_f = T([8], F32, "sc_f")
    sc_b = T([8], BF16, "sc_b")
    spt = [T([4], F32, f"spt{i}") for i in range(4)]
    cp = T([4], F32, "cp")
    chalf = T([4], F32, "chalf")
    hgab = T([4], F32, "hgab")
    hgxb = T([4], F32, "hgxb")
    ss = [T([1], F32, f"ss{i}") for i in range(4)]
    rstd = [T([1], F32, f"rstd{i}") for i in range(4)]
    hcarry = [T([1], F32, f"hc{c}") for c in range(4)]
    haloX = [T([3], F32, f"hx{c}") for c in range(4)]
    chalo = [T([2], F32, f"ch{c}") for c in range(4)]
    eps4 = T([4, 16], F32, "eps4")
    mhalf = T([4, 16], F32, "mhalf")
    mhalf1 = T([1], F32, "mhalf1")
    mask2 = T([2], BF16, "mask2")
    Efull = carve([8, 128], BF16, "E")
    E = Tile(Efull, "E")
    pidx = T([1], F32, "pidx")
    iot = T([128], F32, "iot")
    dg = [T([128], F32, f"dg{i}") for i in range(2)]
    gst = [T([128], F32, f"gst{i}") for i in range(2)]

    win_full = carve([8, DIN], BF16, "win")
    Win = [Tile(win_full[:, :, g * 256:(g + 1) * 256], f"win{g}") for g in range(10)]
    PERM_END = cur[0]

    MIXR = cur[0]
    hT = []
    hT_off = []
    for i in range(2):
        hT_off.append(cur[0])
        apf = carve([8, 512], BF16, f"hT{i}")
        hT.append([Tile(apf[:, k, :], f"hT{i}_{k}") for k in range(8)])
    G2_OFF = cur[0]
    apf = carve([8, 512], BF16, "yT")
    yT = [Tile(apf[:, k, :], f"yT_{k}") for k in range(8)]
    pX = RR([T([516], F32, f"X{i}") for i in range(2)])
    pXL = RR([T([512], F32, f"XL{i}") for i in range(2)])
    pR = RR([T([512], F32, f"R{i}") for i in range(2)])
    pA = RR([T([512], F32, f"A{i}") for i in range(2)])
    G4_OFF = cur[0]
    pI = RR([T([512], F32, f"I{i}") for i in range(2)])
    pUL = RR([T([512], F32, f"UL{i}") for i in range(2)])
    pSQ = RR([T([512], F32, f"SQ{i}") for i in range(2)])
    pY = RR([T([512], F32, f"Y{i}") for i in range(10)])
    pXLB = RR([T([512], BF16, f"XLB{i}") for i in range(2)])
    pYSQ = RR([T([512], BF16, f"YSQ{i}") for i in range(2)])
    RS = T([4, 128], F32, "RS")
    RHI = T([512], BF16, "RHI")
    RLO = T([512], BF16, "RLO")
    MIXR_END = cur[0]
    mixG = [hT[0], hT[1], yT,
            pX.tiles + pXL.tiles + pR.tiles + pA.tiles,
            pI.tiles + pUL.tiles + pSQ.tiles + pY.tiles + pXLB.tiles + pYSQ.tiles + [RS, RHI, RLO]]
    W1e = [None] * 3
    W2e = [None] * 3
    h2T_b0, nb = carve_at(hT_off[0], [8, 512], BF16, "h2Tb0")
    ap_, nb = carve_at(hT_off[1], [8, 512], BF16, "w1e0")
    W1e[0] = Tile(ap_, "w1e0")
    o = G2_OFF
    h1T = []
    for i in range(2):
        ap_, nb = carve_at(o, [4, 512], BF16, f"h1T{i}"); o += nb
        h1T.append([Tile(ap_[:, j, :], f"h1T{i}_{j}") for j in range(4)])
    h2T_b1, nb = carve_at(o, [8, 512], BF16, "h2Tb1"); o += nb
    h2T = [[Tile(h2T_b0[:, k, :], f"h2T{k}_0"), Tile(h2T_b1[:, k, :], f"h2T{k}_1")] for k in range(8)]
    ap_, nb = carve_at(o, [4, 1024], BF16, "w2e0"); o += nb
    W2e[0] = [Tile(ap_[:, kk, :], f"w2e0_{kk}") for kk in range(4)]
    assert o <= G4_OFF, (o, G4_OFF)
    o = G4_OFF
    for r in (1, 2):
        ap_, nb = carve_at(o, [8, 512], BF16, f"w1e{r}"); o += nb
        W1e[r] = Tile(ap_, f"w1e{r}")
        ap_, nb = carve_at(o, [4, 1024], BF16, f"w2e{r}"); o += nb
        W2e[r] = [Tile(ap_[:, kk, :], f"w2e{r}_{kk}") for kk in range(4)]
    pRl_tiles = []
    for i in range(3):
        ap_, nb = carve_at(o, [512], F32, f"Rl{i}"); o += nb
        pRl_tiles.append(Tile(ap_, f"Rl{i}"))
    pRl = RR(pRl_tiles)
    assert o <= MIXR_END, (o, MIXR_END)
    mlpG = [[h2T[k][0] for k in range(8)], [W1e[0]], [t for hh in h1T for t in hh],
            [h2T[k][1] for k in range(8)] + W2e[0],
            [W1e[1], W1e[2]] + W2e[1] + W2e[2] + pRl_tiles]
    WINR = WINR_END = 0
    if verbose:
        print(f"[sbuf] perm={PERM_END} winr={WINR_END-WINR} mixr={MIXR_END-MIXR} total={cur[0]} / {ARENA_BYTES}", flush=True)
    assert cur[0] <= ARENA_BYTES

    def alias(new_tiles, old_tiles):
        acc = set()
        for t in old_tiles:
            acc.update(t.readers)
            if t.last_w is not None:
                acc.add(t.last_w)
        for t in new_tiles:
            a2 = set(acc)
            a2.update(t.readers)
            if t.last_w is not None:
                a2.add(t.last_w)
            t.readers = list(a2)
            t.last_w = None

    banks = [S.psum(f"pb{i}", [128, 512], F32) for i in range(8)]
    mm = RR([Tile(banks[i][:, :], f"mm{i}", excl=True) for i in range(7)])
    tr = mm
    psS_full = banks[7][:, 0:64].rearrange("p (a b) -> p a b", a=4)
    psS = Tile(psS_full, "psS", excl=True)

    MULT, ADD, SUB = ALU.mult, ALU.add, ALU.subtract

    S.dma(small.ap, small_d, writes=[small], key="small", nbytes=128 * NS * 4)

    win_done = []
    ada_first = []

    def load_x(t, after=()):
        xt = xbuf[t % 8]
        S.dma(xt.ap, x_d[t * 128:(t + 1) * 128, :], writes=[xt], key=f"x{t % 8}", nbytes=128 * 1024 * 4, name=f"ldx{t}", after=after)

    for t in range(4):
        load_x(t)

    add("pool", lambda e: e.iota(iot.ap, [[1, 128]], 0, channel_multiplier=0, allow_small_or_imprecise_dtypes=True), [], [iot], c_pool(128))
    add("pool", lambda e: e.iota(pidx.ap, [[0, 1]], 0, channel_multiplier=1, allow_small_or_imprecise_dtypes=True), [], [pidx], c_pool(1))
    add("dve", lambda e: e.tensor_scalar(ident.ap, iot.ap, pidx.ap, None, ALU.is_equal), [iot, pidx], [ident], c_dve(128))
    add("pool", lambda e: e.memset(ones.ap, 1.0), [], [ones], c_pool(128))
    add("pool", lambda e: e.memset(mhalf.ap, -0.5), [], [mhalf], c_pool(64))
    add("pool", lambda e: e.memset(mhalf1.ap, -0.5), [], [mhalf1], c_pool(1))
    add("pool", lambda e: e.memset(eps4[:, :, 0:8], 4.0 * EPS), [], [eps4], c_pool(32))
    add("pool", lambda e: e.memset(eps4[:, :, 8:16], EPS), [eps4], [eps4], c_pool(32))
    add("pool", lambda e: e.memset(mask2.ap, 0.0), [], [mask2], c_pool(2))
    add("pool", lambda e: e.memset(mask2[0:64, 0:1], 1.0 / 64), [mask2], [mask2], c_pool(1))
    add("pool", lambda e: e.memset(mask2[64:128, 1:2], 1.0 / 64), [mask2], [mask2], c_pool(1))
    for c in range(4):
        add("pool", lambda e, c=c: e.memset(hcarry[c].ap, 0.0), [], [hcarry[c]], c_pool(1))
        add("pool", lambda e, c=c: e.memset(haloX[c].ap, 0.0), [], [haloX[c]], c_pool(3))
        add("pool", lambda e, c=c: e.memset(chalo[c].ap, 0.0), [], [chalo[c]], c_pool(2))
    ef = stg[1]
    ef_v = ef.ap.rearrange("p (a b c) -> p a b c", a=8, b=2)[:, :, :, 0:64]
    add("pool", lambda e: e.iota(ef_v, [[-2, 8], [-1, 2], [0, 64]], 0, channel_multiplier=1,
                                 allow_small_or_imprecise_dtypes=True), [], [ef], c_pool(1024))
    add("dve", lambda e: e.tensor_single_scalar(E.ap.rearrange("p a (b c) -> p a b c", b=2), ef_v, 0.0, ALU.is_equal),
        [ef], [E], c_dve(1024))

    ccol = small[:, C_C:C_C + 8]
    add("act", lambda e: e.activation(sc_th.ap, ccol, AF.Tanh, scale=0.5), [small], [sc_th], c_act(8)).group = "e"
    add("dve", lambda e: e.tensor_scalar(sc_th.ap, sc_th.ap, 0.5, 0.5, MULT, ADD), [sc_th], [sc_th], c_dve(8))
    add("dve", lambda e: e.tensor_tensor(sc_f.ap, sc_th.ap, ccol, MULT), [sc_th, small], [sc_f], c_dve(8))
    add("dve", lambda e: e.tensor_copy(sc_b.ap, sc_f.ap), [sc_f], [sc_b], c_dve(8))

    apc = small[:, C_AP:C_AP + 4]
    add("act", lambda e: e.activation(spt[0].ap, apc, AF.Abs), [small], [spt[0]], c_act(4))
    add("act", lambda e: e.activation(spt[1].ap, spt[0].ap, AF.Exp, scale=-1.0), [spt[0]], [spt[1]], c_act(4)).group = "e"
    add("act", lambda e: e.activation(spt[2].ap, spt[1].ap, AF.Ln, bias=1.0), [spt[1]], [spt[2]], c_act(4)).group = "l"
    add("dve", lambda e: e.tensor_scalar_max(spt[3].ap, apc, 0.0), [small], [spt[3]], c_dve(4))
    add("dve", lambda e: e.tensor_tensor(spt[3].ap, spt[3].ap, spt[2].ap, ADD), [spt[3], spt[2]], [spt[3]], c_dve(4))
    add("dve", lambda e: e.tensor_scalar(cp.ap, spt[3].ap, -8.0, None, MULT), [spt[3]], [cp], c_dve(4))
    add("dve", lambda e: e.tensor_scalar(chalf.ap, spt[3].ap, -4.0, None, MULT), [spt[3]], [chalf], c_dve(4))
    add("dve", lambda e: e.tensor_scalar(hgab.ap, small[:, C_GAB:C_GAB + 4], 0.5, None, MULT), [small], [hgab], c_dve(4))
    add("dve", lambda e: e.tensor_scalar(hgxb.ap, small[:, C_GXB:C_GXB + 4], 0.5, None, MULT), [small], [hgxb], c_dve(4))

    stg_rr = RR(stg)

    def ada_piece(p):
        st = stg_rr.next()
        si = stg.index(st)
        src = adaw_d[:, p * 512:(p + 1) * 512].rearrange("(k q) c -> q k c", q=128)
        o_ = S.dma(stg_bf[si], src, writes=[st], key=f"stg{si}", nbytes=2 * 1024 * 1024, eng="pool", name=f"ada{p}",
                   after=(win_done if p >= 4 else ()))
        if p < 4:
            ada_first.append(o_.id)
        ps = mm.next()
        for jj in range(4):
            for k in range(8):
                add("pe", lambda e, si=si, jj=jj, k=k, ps=ps: e.matmul(
                    ps[:, jj:jj + 1], stg_bf[si][:, k, jj * 128:(jj + 1) * 128], sc_b[:, k:k + 1],
                    start=(k == 0), stop=(k == 7)), [st, sc_b], [ps], c_mm(64) + 30)
        j0 = p * 4
        mt, mo = mc[p // 2], (p % 2) * 4
        add("dve", lambda e, ps=ps, j0=j0, mt=mt, mo=mo: e.tensor_tensor(mt[:, mo:mo + 4], ps[:, 0:4],
                                                                       small[:, C_ADAB + j0:C_ADAB + j0 + 4], ADD),
            [ps, small] + ([mt] if mo else []), [mt], c_dve(4, psum=True))

    def make_gs(gs, sct, g_c0):
        add("dve", lambda e: e.scalar_tensor_tensor(gs.ap, sct.ap, 1.0,
                                                    small[:, g_c0:g_c0 + 8], ADD, MULT), [sct, small], [gs], c_dve(8))

    def load_bc(col_tile, c0):
        for half in range(2):
            ps = mm.next()
            for kk in range(4):
                k = half * 4 + kk
                d = dg[k % 2]
                add("dve", lambda e, d=d, k=k: e.tensor_scalar(d.ap, ident.ap, col_tile[:, c0 + k:c0 + k + 1], None, MULT),
                    [ident, col_tile], [d], c_dve(128))
                add("pe", lambda e, d=d, kk=kk, ps=ps: e.matmul(ps[:, kk * 128:(kk + 1) * 128], ones.ap, d.ap,
                                                               start=True, stop=True), [ones, d], [ps], c_mm(128, fp32=True))
            add("act", lambda e, ps=ps, half=half: e.activation(gate_bc[:, half * 512:(half + 1) * 512], ps.ap, AF.Copy),
                [ps], [gate_bc], c_act(512, psum=True))

    for p in range(4):
        ada_piece(p)
    make_gs(gs1, mc[1], C_N1G)

    def load_win():
        for g in (0, 2, 8, 6, 4, 1, 3, 9, 7, 5):
            src = win_d[:, g * 256:(g + 1) * 256].rearrange("(k q) c -> q k c", q=128)
            o_ = S.dma(Win[g].ap, src, writes=[Win[g]], key=f"win{g}", nbytes=1024 * 1024, eng="pool", name=f"win{g}",
                       after=(ada_first[:2] if g in (0, 2) else ada_first))
            win_done.append(o_.id)

    load_win()
    for t in range(4, 8):
        load_x(t, after=win_done)

    gst_rr = RR(gst)
    for c in range(4):
        for (wd, Wt, nm) in ((gaw_d, Wa, "a"), (gxw_d, Wx, "x")):
            g = gst_rr.next()
            gi = gst.index(g)
            add("pool", lambda e, g=g: e.memset(g.ap, 0.0), [], [g], c_pool(128))
            S.dma(g[0:64, 0:64], wd[(2 * c) * 64:(2 * c + 1) * 64, :], writes=[g], key=f"gst{gi}", nbytes=16384)
            S.dma(g[64:128, 64:128], wd[(2 * c + 1) * 64:(2 * c + 2) * 64, :], writes=[g], key=f"gst{gi}", nbytes=16384)
            add("pool", lambda e, g=g, Wt=Wt, c=c: e.tensor_copy(Wt[c].ap, g.ap), [g], [Wt[c]], c_pool(128))

    def load_wout():
        for i in range(4):
            st = stg_rr.next()
            si = stg.index(st)
            src = wout_d[i * 256:(i + 1) * 256, :].rearrange("(k q) c -> q k c", q=128)
            S.dma(stg_f3[si], src, writes=[st], key=f"stg{si}", nbytes=1024 * 1024, name=f"wout{i}", after=win_done)
            for j in range(2):
                k = i * 2 + j
                add("pool", lambda e, si=si, j=j, k=k: e.tensor_tensor(Wout[k].ap, stg_f3[si][:, j, :], gate_bc.ap, MULT),
                    [st, gate_bc], [Wout[k]], c_pool(1024))


    def rmsnorm_T(tiles, gs, sht, dst, ncol0):
        for pair in range(2):
            tl = tiles[pair * 2:pair * 2 + 2]
            for i, xt in enumerate(tl):
                xs = xn[i]
                add("act", lambda e, xs=xs, xt=xt, i=i: e.activation(xs.ap, xt.ap, AF.Square, accum_out=ss[i].ap),
                    [xt], [xs, ss[i]], c_act(1024) + 80)
                add("pool", lambda e, i=i: e.tensor_scalar(rstd[i].ap, ss[i].ap, 1.0 / D, EPS, MULT, ADD), [ss[i]], [rstd[i]], c_pool(1))
                add("pool", lambda e, i=i: e.tensor_tensor(rstd[i].ap, rstd[i].ap, mhalf1.ap, ALU.pow), [rstd[i], mhalf1], [rstd[i]], 1500)
                add("act", lambda e, xs=xs, xt=xt, i=i: e.activation(xs.ap, xt.ap, AF.Copy, scale=rstd[i].ap),
                    [xt, rstd[i]], [xs], c_act(1024) + 80)
            for q in range(4):
                bank = tr.next()
                for kk in range(2):
                    k = 2 * q + kk
                    for i in range(2):
                        add("pe", lambda e, bank=bank, kk=kk, i=i, k=k: e.transpose(
                            bank[:, (kk * 2 + i) * 128:(kk * 2 + i + 1) * 128], xn[i][:, k * 128:(k + 1) * 128], ident.ap),
                            [xn[i], ident], [bank], c_mm(128) + 20)
                for kk in range(2):
                    k = 2 * q + kk
                    dt_ = dst[k]
                    oap = dt_[:, ncol0 + pair * 256: ncol0 + (pair + 1) * 256] if ncol0 is not None else None
                    if q == 0:
                        add("act", lambda e, bank=bank, kk=kk, k=k, dt_=dt_, pair=pair: e.activation(
                            dt_[:, pair * 256:(pair + 1) * 256], bank[:, kk * 256:(kk + 1) * 256], AF.Identity,
                            bias=sht[:, k:k + 1], scale=gs[:, k:k + 1]),
                            [bank, sht, gs], [dt_], c_act(256, psum=True) + 150)
                    else:
                        add("dve", lambda e, bank=bank, kk=kk, k=k, dt_=dt_, pair=pair: e.tensor_scalar(
                            dt_[:, pair * 256:(pair + 1) * 256], bank[:, kk * 256:(kk + 1) * 256],
                            gs[:, k:k + 1], sht[:, k:k + 1], MULT, ADD),
                            [bank, sht, gs], [dt_], c_dve(256, psum=True))

    def inproj(hTb, m):
        ps = mm.next()
        g, off = m // 2, (m % 2) * 128
        for k in range(8):
            add("pe", lambda e, ps=ps, g=g, off=off, k=k: e.matmul(ps.ap, Win[g][:, k, off:off + 128], hTb[k].ap,
                                                                   start=(k == 0), stop=(k == 7)),
                [Win[g], hTb[k]], [ps], c_mm(512))
        return ps

    def act(fn, reads, writes, cost, group=None):
        o_ = add("act", fn, reads, writes, cost)
        o_.group = group
        return o_

    def lru_A(gb, hTb, c):
        ps_lx = inproj(hTb, c)
        X = pX.next()
        act(lambda e: e.activation(X[:, 3:515], ps_lx.ap, AF.Copy), [ps_lx], [X], c_act(512, True))
        add("dve", lambda e: e.tensor_copy(X[:, 0:3], haloX[c].ap), [haloX[c], X], [X], c_dve(3))
        add("dve", lambda e: e.tensor_copy(haloX[c].ap, X[:, 512:515]), [X], [haloX[c]], c_dve(3))
        ps_ly = inproj(hTb, 4 + c)
        UL = pUL.next()
        SQ = pSQ.next()
        act(lambda e: e.activation(UL.ap, ps_ly.ap, AF.Copy), [ps_ly], [UL], c_act(512, True))
        act(lambda e: e.activation(SQ.ap, ps_ly.ap, AF.Square, scale=0.044715 ** 0.5), [ps_ly], [SQ], c_act(512, True))
        XL = pXL.next()
        w = lambda k: small[:, C_LCW + c * 4 + k:C_LCW + c * 4 + k + 1]
        add("dve", lambda e: e.tensor_scalar(XL.ap, X[:, 3:515], w(3), small[:, C_LCB + c:C_LCB + c + 1], MULT, ADD),
            [X, small], [XL], c_dve(512, 2))
        for k in (2, 1, 0):
            add("dve", lambda e, k=k: e.scalar_tensor_tensor(XL.ap, X[:, k:k + 512], w(k), XL.ap, MULT, ADD),
                [X, small, XL], [XL], c_dve(512))
        XLB = pXLB.next()
        add("dve", lambda e: e.tensor_copy(XLB.ap, XL.ap), [XL], [XLB], c_dve(512, 2))
        ps_a = mm.next()
        add("pe", lambda e: e.matmul(ps_a.ap, Wa[c].ap, XLB.ap, start=True, stop=True), [Wa[c], XLB], [ps_a], c_mm(512))
        ps_x = mm.next()
        add("pe", lambda e: e.matmul(ps_x.ap, Wx[c].ap, XLB.ap, start=True, stop=True), [Wx[c], XLB], [ps_x], c_mm(512))
        R = pR.next()
        I = pI.next()
        act(lambda e: e.activation(R.ap, ps_a.ap, AF.Tanh, bias=hgab[:, c:c + 1], scale=0.5), [ps_a, hgab], [R], c_act(512, True) + 80, "e")
        act(lambda e: e.activation(I.ap, ps_x.ap, AF.Tanh, bias=hgxb[:, c:c + 1], scale=0.5), [ps_x, hgxb], [I], c_act(512, True) + 80, "e")
        return dict(c=c, gb=gb, XL=XL, UL=UL, SQ=SQ, R=R, I=I)

    def lru_B(ctx, Ys):
        c, gb, XL, UL, SQ, R, I = (ctx[k] for k in ("c", "gb", "XL", "UL", "SQ", "R", "I"))
        A = pA.next()
        act(lambda e: e.activation(A.ap, R.ap, AF.Exp, bias=chalf[:, c:c + 1], scale=chalf[:, c:c + 1]), [R, chalf], [A], c_act(512) + 160, "e")
        act(lambda e: e.activation(R.ap, R.ap, AF.Exp, bias=cp[:, c:c + 1], scale=cp[:, c:c + 1]), [R, cp], [R], c_act(512) + 160, "e")
        act(lambda e: e.activation(R.ap, R.ap, AF.Sqrt, bias=0.25 + 2e-6, scale=-0.25), [R], [R], c_act(512), "s")
        if gb == 0:
            add("pool", lambda e: e.memset(R[:, 0:1], 0.5), [R], [R], c_pool(1))
        add("dve", lambda e: e.scalar_tensor_tensor(I.ap, I.ap, 1.0, XL.ap, ADD, MULT), [I, XL], [I], c_dve(512))
        add("dve", lambda e: e.tensor_tensor(I.ap, R.ap, I.ap, MULT), [R, I], [I], c_dve(512))
        H = XL
        add("dve", lambda e: e.tensor_tensor_scan(H.ap, A.ap, I.ap, hcarry[c].ap, MULT, ADD),
            [A, I, hcarry[c]], [H], (150 + 1024) / 0.96)
        add("dve", lambda e: e.tensor_copy(hcarry[c].ap, H[:, 511:512]), [H], [hcarry[c]], c_dve(1))
        add("dve", lambda e: e.scalar_tensor_tensor(SQ.ap, SQ.ap, 1.0, UL.ap, ADD, MULT), [SQ, UL], [SQ], c_dve(512))
        act(lambda e: e.activation(SQ.ap, SQ.ap, AF.Tanh, scale=GELU_C), [SQ], [SQ], c_act(512), "e")
        add("dve", lambda e: e.scalar_tensor_tensor(SQ.ap, SQ.ap, 1.0, UL.ap, ADD, MULT), [SQ, UL], [SQ], c_dve(512))
        Y = pY.next()
        add("dve", lambda e: e.tensor_tensor(Y.ap, SQ.ap, H.ap, MULT), [SQ, H], [Y], c_dve(512))
        Ys[c] = Y
        stats(Y, c)

    def stats(Y, hc):
        YSQ = pYSQ.next()
        act(lambda e: e.activation(YSQ.ap, Y.ap, AF.Square), [Y], [YSQ], c_act(512))
        for t in range(4):
            add("pe", lambda e, t=t: e.matmul(psS[:, t, 2 * hc:2 * hc + 2], YSQ[:, t * 128:(t + 1) * 128], mask2.ap,
                                              start=True, stop=True), [YSQ, mask2], [psS], c_mm(64) + 40)

    def conv_chunk(gb, hTb, c, Ys):
        ps_v = inproj(hTb, 16 + c)
        UV = pA.next()
        act(lambda e: e.activation(UV.ap, ps_v.ap, AF.Copy), [ps_v], [UV], c_act(512, True))
        ps_c = inproj(hTb, 12 + c)
        CV = pX.next()
        add("dve", lambda e: e.tensor_tensor(CV[:, 2:514], ps_c.ap, UV.ap, MULT), [ps_c, UV], [CV], c_dve(512, psum=True))
        add("dve", lambda e: e.tensor_copy(CV[:, 0:2], chalo[c].ap), [chalo[c], CV], [CV], c_dve(2))
        add("dve", lambda e: e.tensor_copy(chalo[c].ap, CV[:, 512:514]), [CV], [chalo[c]], c_dve(2))
        ps_b = inproj(hTb, 8 + c)
        CO = UV
        w = lambda k: small[:, C_SCW + c * 3 + k:C_SCW + c * 3 + k + 1]
        add("dve", lambda e: e.tensor_scalar(CO.ap, CV[:, 2:514], w(2), None, MULT), [CV, small], [CO], c_dve(512, 2))
        for k in (1, 0):
            add("dve", lambda e, k=k: e.scalar_tensor_tensor(CO.ap, CV[:, k:k + 512], w(k), CO.ap, MULT, ADD),
                [CV, small, CO], [CO], c_dve(512))
        Y = pY.next()
        add("dve", lambda e: e.tensor_tensor(Y.ap, ps_b.ap, CO.ap, MULT), [ps_b, CO], [Y], c_dve(512, psum=True))
        Ys[4 + c] = Y
        stats(Y, 4 + c)

    def norm_apply(Ys):
        add("dve", lambda e: e.tensor_tensor(RS[:, :, 0:16], psS.ap, eps4.ap, ADD), [psS, eps4, RS], [RS], c_dve(64, psum=True))
        act(lambda e: e.activation(RS[:, :, 0:16], RS[:, :, 0:16], AF.Sqrt), [RS], [RS], c_act(64), "s")
        add("dve", lambda e: e.reciprocal(RS[:, :, 0:16], RS[:, :, 0:16]), [RS], [RS], (150 + 8 * 64) / 0.96)
        bank = tr.next()
        for t in range(4):
            add("pe", lambda e, t=t: e.transpose(bank[:, t * 128:(t + 1) * 128], RS[:, t, :], ident.ap), [RS, ident], [bank], c_mm(128) + 20)
        act(lambda e: e.activation(RHI.ap, bank.ap, AF.Copy), [bank], [RHI], c_act(512, True))
        add("dve", lambda e: e.tensor_tensor(RLO.ap, bank.ap, RHI.ap, SUB), [bank, RHI], [RLO], c_dve(512, psum=True))
        for c8 in range(8):
            ps = mm.next()
            add("pe", lambda e, ps=ps, c8=c8: e.matmul(ps.ap, E[:, c8, :], RHI.ap, start=True, stop=False), [E, RHI], [ps], c_mm(512))
            add("pe", lambda e, ps=ps, c8=c8: e.matmul(ps.ap, E[:, c8, :], RLO.ap, start=False, stop=True), [E, RLO], [ps], c_mm(512))
            gcol = (C_LOG + c8) if c8 < 4 else (C_COG + c8 - 4)
            Y = Ys[c8]
            add("dve", lambda e, ps=ps, Y=Y, gcol=gcol, c8=c8: e.scalar_tensor_tensor(
                yT[c8].ap, Y.ap, small[:, gcol:gcol + 1], ps.ap, MULT, MULT), [Y, small, ps], [yT[c8]], c_dve(512, psum=True))

    def outproj(tiles):
        for xt in tiles:
            ti = tiles.index(xt)
            for nh in range(2):
                ps = mm.next()
                for k in range(8):
                    add("pe", lambda e, ps=ps, k=k, ti=ti, nh=nh: e.matmul(
                        ps.ap, yT[k][:, ti * 128:(ti + 1) * 128], Wout[k][:, nh * 512:(nh + 1) * 512],
                        start=(k == 0), stop=(k == 7)), [yT[k], Wout[k]], [ps], c_mm(512))
                add("dve", lambda e, ps=ps, xt=xt, nh=nh: e.tensor_tensor(
                    xt[:, nh * 512:(nh + 1) * 512], xt[:, nh * 512:(nh + 1) * 512], ps.ap, ADD), [xt, ps], [xt], c_dve(512, psum=True))

    def final_tiles(half, i8s):
        for i8 in i8s:
            t = half * 8 + i8
            xt = xbuf[t % 8]
            i = i8 % 4
            xs = xn[i % 2]
            add("act", lambda e, xs=xs, xt=xt, i=i: e.activation(xs.ap, xt.ap, AF.Square, accum_out=ss[i].ap),
                [xt], [xs, ss[i]], c_act(1024) + 80)
            add("pool", lambda e, i=i: e.tensor_scalar(rstd[i].ap, ss[i].ap, 1.0 / D, EPS, MULT, ADD), [ss[i]], [rstd[i]], c_pool(1))
            add("pool", lambda e, i=i: e.tensor_tensor(rstd[i].ap, rstd[i].ap, mhalf1.ap, ALU.pow), [rstd[i], mhalf1], [rstd[i]], 1500)
            add("dve", lambda e, xt=xt, i=i: e.scalar_tensor_tensor(xt.ap, xt.ap, rstd[i].ap, gate_bc.ap, MULT, MULT),
                [xt, rstd[i], gate_bc], [xt], c_dve(1024))
            st_ = S.dma(y_d[t * 128:(t + 1) * 128, :], xt.ap, reads=[xt], key=f"x{t % 8}", nbytes=128 * 1024 * 4, name=f"st{t}")
            stores.append(st_.id)
            if half == 0:
                load_x(t + 8)

    w1_rr = [0]
    h1_rr = [0]
    stores = []
    if verbose:
        print("[mark] prologue end", len(S.ops))
    for half in range(2):
        if verbose:
            print("[mark] half", half, "start", len(S.ops))
        if half == 1:
            for gi in range(len(mixG)):
                alias(mixG[gi], mlpG[gi])
        w1_rr[0] = 0
        add("pool", lambda e: e.memset(RS.ap, 0.0), [], [RS], c_pool(512))
        blocks = []
        for bi in range(2):
            gb = half * 2 + bi
            blocks.append(dict(gb=gb, tiles=[xbuf[(gb * 4 + i) % 8] for i in range(4)], hTb=hT[gb % 2],
                               Ys=[None] * 8, ctxs=[None] * 4))

        def start_block(B):
            rmsnorm_T(B["tiles"], gs1, mc[0], B["hTb"], None)
            B["ctxs"][0] = lru_A(B["gb"], B["hTb"], 0)
            B["ctxs"][1] = lru_A(B["gb"], B["hTb"], 1)

        def chunk_loop(B, after_c1=None):
            gb, hTb, Ys, ctxs = B["gb"], B["hTb"], B["Ys"], B["ctxs"]
            for c in range(4):
                lru_B(ctxs[c], Ys)
                conv_chunk(gb, hTb, c, Ys)
                if c + 2 < 4:
                    ctxs[c + 2] = lru_A(gb, hTb, c + 2)
                if c == 1 and after_c1 is not None:
                    after_c1()
                if gb == 1:
                    ada_piece(6 + c)
                    if c >= 2:
                        ada_piece(8 + c)

        B0, B1 = blocks
        if verbose and half == 0:
            print("[mark] gb0 norm start", len(S.ops))
        start_block(B0)
        chunk_loop(B0)
        if half == 0:
            for p in (4, 5):
                ada_piece(p)
            load_bc(mc[2], 0)
        start_block(B1)
        if half == 0:
            load_wout()
        if verbose and half == 0:
            print("[mark] gb0 norm_apply", len(S.ops))
        norm_apply(B0["Ys"])
        if verbose and half == 0:
            print("[mark] gb0 end", len(S.ops))
        chunk_loop(B1, after_c1=lambda: outproj(B0["tiles"]))
        if half == 0:
            make_gs(gs2, mc[4], C_N2G)
            load_bc(mc[5], 0)
        alias(mlpG[0], mixG[0])
        rmsnorm_T(B0["tiles"], gs2, mc[3], [h2T[k][0] for k in range(8)], None)
        norm_apply(B1["Ys"])
        outproj(B1["tiles"])
        tiles_prev = B1["tiles"]
        if verbose:
            print("[mark] half", half, "mlp start", len(S.ops))
        for gi in range(1, len(mixG)):
            alias(mlpG[gi], mixG[gi])
        rmsnorm_T(tiles_prev, gs2, mc[3], [h2T[k][1] for k in range(8)], None)
        for e8 in range(8):
            r = w1_rr[0] % 3
            w1_rr[0] += 1
            src = w1_d[:, e8 * 512:(e8 + 1) * 512].rearrange("(k q) c -> q k c", q=128)
            S.dma(W1e[r].ap, src, writes=[W1e[r]], key=f"w1e{r}", nbytes=2 * 1024 * 1024, eng="pool", name=f"w1e{e8}")
            for piece in range(2):
                st = stg_rr.next()
                si = stg.index(st)
                src = w2_d[e8 * 512 + piece * 256:e8 * 512 + (piece + 1) * 256, :].rearrange("(k q) c -> q k c", q=128)
                S.dma(stg_f3[si], src, writes=[st], key=f"stg{si}", nbytes=1024 * 1024, name=f"w2e{e8}_{piece}")
                for j in range(2):
                    kk = piece * 2 + j
                    add("pool", lambda e, si=si, j=j, r=r, kk=kk: e.tensor_tensor(W2e[r][kk].ap, stg_f3[si][:, j, :], gate_bc.ap, MULT),
                        [st, gate_bc], [W2e[r][kk]], c_pool(1024))
            if e8 == 7:
                load_bc(small, C_FG)
            for bi in range(2):
                gb = half * 2 + bi
                tiles = [xbuf[(gb * 4 + i) % 8] for i in range(4)]
                hb = h1T[h1_rr[0] % 2]
                h1_rr[0] += 1
                for j in range(4):
                    ps = mm.next()
                    for k in range(8):
                        add("pe", lambda e, ps=ps, r=r, k=k, j=j, bi=bi: e.matmul(
                            ps.ap, W1e[r][:, k, j * 128:(j + 1) * 128], h2T[k][bi].ap, start=(k == 0), stop=(k == 7)),
                            [W1e[r], h2T[k][bi]], [ps], c_mm(512))
                    Rl = pRl.next()
                    act(lambda e, ps=ps, Rl=Rl: e.activation(Rl.ap, ps.ap, AF.Relu), [ps], [Rl], c_act(512, True))
                    add("dve", lambda e, ps=ps, Rl=Rl, hb=hb, j=j: e.tensor_tensor(hb[j].ap, ps.ap, Rl.ap, MULT),
                        [ps, Rl], [hb[j]], c_dve(512, psum=True))
                for ti, xt in enumerate(tiles):
                    for nh in range(2):
                        ps = mm.next()
                        for kk in range(4):
                            add("pe", lambda e, ps=ps, kk=kk, ti=ti, nh=nh, hb=hb, r=r: e.matmul(
                                ps.ap, hb[kk][:, ti * 128:(ti + 1) * 128], W2e[r][kk][:, nh * 512:(nh + 1) * 512],
                                start=(kk == 0), stop=(kk == 3)), [hb[kk], W2e[r][kk]], [ps], c_mm(512))
                        add("dve", lambda e, ps=ps, xt=xt, nh=nh: e.tensor_tensor(
                            xt[:, nh * 512:(nh + 1) * 512], xt[:, nh * 512:(nh + 1) * 512], ps.ap, ADD), [xt, ps], [xt], c_dve(512, psum=True))
                if e8 == 7:
                    final_tiles(half, range(bi * 4, bi * 4 + 4))
        if half == 0:
            load_bc(mc[5], 0)

    stores = [i for i in stores if i >= 0]
    if S.cut is not None:
        for t in range(8):
            st_ = S.dma(y_d[t * 128:(t + 1) * 128, :], xbuf[t].ap, reads=[xbuf[t]], key=f"dbg{t}", nbytes=128 * 1024 * 4, name=f"dbg{t}")
            stores.append(st_.id)
    S.simulate(verbose=verbose)
    S.emit(final_waits=stores)
    S.close()
    return nc


_CACHE = {}


def _pack_small(inp, b):
    col = lambda v: np.ascontiguousarray(np.asarray(v, np.float32).reshape(-1, 128).T)
    sm = np.zeros((128, NS), np.float32)
    sm[:, C_C:C_C + 8] = col(inp["c"][b])
    sm[:, C_ADAB:C_ADAB + 48] = col(inp["ada_b"][0])
    sm[:, C_N1G:C_N1G + 8] = col(inp["norm1_g"][0])
    sm[:, C_N2G:C_N2G + 8] = col(inp["norm2_g"][0])
    lcw = np.asarray(inp["lru_conv_w"][0], np.float32)
    for c in range(4):
        for k in range(4):
            sm[:, C_LCW + c * 4 + k] = lcw[k, c * 128:(c + 1) * 128]
    sm[:, C_LCB:C_LCB + 4] = col(inp["lru_conv_b"][0])
    sm[:, C_GAB:C_GAB + 4] = col(inp["gate_a_b"][0])
    sm[:, C_GXB:C_GXB + 4] = col(inp["gate_x_b"][0])
    sm[:, C_AP:C_AP + 4] = col(inp["a_param"][0])
    sm[:, C_LOG:C_LOG + 4] = col(inp["lru_out_g"][0])
    sm[:, C_COG:C_COG + 4] = col(inp["conv_out_g"][0])
    scw = np.asarray(inp["short_conv_w"][0], np.float32)
    for c in range(4):
        for k in range(3):
            sm[:, C_SCW + c * 3 + k] = scw[k, c * 128:(c + 1) * 128]
    sm[:, C_FG:C_FG + 8] = col(inp["final_g"])
    return sm


def kernel(**inputs):
    inp = {k: np.asarray(v) for k, v in inputs.items()}
    if "nc" not in _CACHE:
        _CACHE["nc"] = build_program(verbose=False)
    nc = _CACHE["nc"]
    f32 = lambda a: np.ascontiguousarray(np.asarray(a, np.float32))
    shared = {
        "ada_w": f32(inp["ada_w"][0]),
        "w_in": f32(inp["w_in"][0]),
        "w_out": f32(inp["w_out"][0]),
        "w_mlp1": f32(inp["w_mlp1"][0]),
        "w_mlp2": f32(inp["w_mlp2"][0]),
        "gaw": f32(inp["gate_a_w"][0]).reshape(512, 64),
        "gxw": f32(inp["gate_x_w"][0]).reshape(512, 64),
    }
    in_maps = []
    for b in range(8):
        m = dict(shared)
        m["x"] = f32(inp["x"][b])
        m["small"] = _pack_small(inp, b)
        in_maps.append(m)
    res = run_bass_kernel_spmd(nc, in_maps, core_ids=list(range(8)))
    out = np.stack([np.asarray(res.results[b]["y"], np.float32) for b in range(8)], axis=0)
    return out
```

```python
import heapq
from contextlib import ExitStack

import concourse.bass as bass
import concourse.mybir as mybir

F32 = mybir.dt.float32
BF16 = mybir.dt.bfloat16
ALU = mybir.AluOpType
AF = mybir.ActivationFunctionType

ENGS = ("pe", "act", "dve", "pool", "sp")
CP_ALPHA = 0.7


class Tile:
    __slots__ = ("ap", "name", "last_w", "readers", "excl")

    def __init__(self, ap, name, excl=False):
        self.ap = ap
        self.name = name
        self.last_w = None
        self.readers = []
        self.excl = excl

    def __getitem__(self, key):
        return self.ap[key]


class Op:
    __slots__ = ("id", "eng", "fn", "deps", "cost", "dma_key", "users", "name",
                 "start", "finish", "pos", "signal", "semval", "nbytes", "group")

    def __init__(self):
        self.users = []
        self.signal = False
        self.semval = 0
        self.dma_key = None
        self.group = None


class Sched:
    def __init__(self, nc):
        self.nc = nc
        self.ops = []
        self.stack = ExitStack()
        self.n_sb = 0
        self.dma_count = {}
        import os
        self.cut = int(os.environ["KCUT"]) if os.environ.get("KCUT") else None

    def sbuf(self, name, shape, dtype):
        h = self.stack.enter_context(self.nc.sbuf_tensor(name, list(shape), dtype))
        return h

    def psum(self, name, shape, dtype):
        h = self.stack.enter_context(self.nc.psum_tensor(name, list(shape), dtype))
        return h

    def tile(self, name, shape, dtype):
        h = self.sbuf(name, shape, dtype)
        return Tile(h[tuple(slice(None) for _ in shape)], name)

    def view(self, ap, name):
        return Tile(ap, name)

    def add(self, eng, fn, reads=(), writes=(), cost=100.0, name="", dma_key=None, nbytes=0, after=()):
        o = Op()
        if self.cut is not None and len(self.ops) >= self.cut and not name.startswith("dbg"):
            o.id = -1
            return o
        o.id = len(self.ops)
        o.eng = eng
        o.fn = fn
        o.cost = float(cost)
        o.name = name
        o.nbytes = nbytes
        deps = set()
        xr = [t for t in reads if t.excl and t not in writes]
        if xr:
            reads = [t for t in reads if not t.excl]
            writes = list(writes) + xr
        for t in reads:
            if t.last_w is not None:
                deps.add(t.last_w)
        for t in writes:
            if t.last_w is not None:
                deps.add(t.last_w)
            deps.update(t.readers)
        deps.update(i for i in after if i is not None and i >= 0)
        deps.discard(o.id)
        o.deps = deps
        if dma_key is not None:
            o.dma_key = dma_key
        for t in reads:
            t.readers.append(o.id)
        for t in writes:
            t.last_w = o.id
            t.readers = []
        self.ops.append(o)
        return o

    def dma(self, out_ap, in_ap, reads=(), writes=(), key=None, nbytes=0, name="", eng="sp", after=(), **kw):
        assert key is not None
        if eng == "pool":
            key = key + "_sw"
        return self.add(eng, lambda e: e.dma_start(out=out_ap, in_=in_ap, **kw), reads, writes,
                        cost=(1100.0 if eng == "pool" else 60.0), name=name or key, dma_key=key, nbytes=nbytes, after=after)

    def simulate(self, dma_bw=450.0, dma_lat=1500.0, sem_lat=100.0, verbose=True):
        ops = self.ops
        n = len(ops)
        for o in ops:
            o.users = []
        for o in ops:
            for d in o.deps:
                ops[d].users.append(o.id)
        remaining = [len(o.deps) for o in ops]
        prio = list(range(n))
        if CP_ALPHA > 0.0:
            cp = [0.0] * n
            for o in reversed(ops):
                best = 0.0
                for u in o.users:
                    if cp[u] > best:
                        best = cp[u]
                cp[o.id] = best + o.cost + (o.nbytes / dma_bw + dma_lat if o.dma_key is not None else 0.0)
            total = max(cp) if n else 1.0
            for o in ops:
                prio[o.id] = (o.id / n) * total * (1.0 - CP_ALPHA) - cp[o.id] * CP_ALPHA
        ready_time = [0.0] * n
        future = {e: [] for e in ENGS}
        avail = {e: [] for e in ENGS}
        eng_free = {e: 0.0 for e in ENGS}
        busy = {e: 0.0 for e in ENGS}
        order = {e: [] for e in ENGS}
        dma_free = 0.0
        act_table = [None]
        for o in ops:
            if remaining[o.id] == 0:
                heapq.heappush(future[o.eng], (0.0, o.id))
        t = 0.0
        nsched = 0
        while nsched < n:
            progressed = False
            for e in ENGS:
                fu = future[e]
                av = avail[e]
                while fu and fu[0][0] <= t:
                    _, i = heapq.heappop(fu)
                    heapq.heappush(av, (prio[i], i))
                if eng_free[e] <= t and av:
                    _, i = heapq.heappop(av)
                    o = ops[i]
                    o.start = t
                    if e == "act" and o.group is not None and o.group != act_table[0]:
                        act_table[0] = o.group
                        o.cost += 1280.0
                    if o.dma_key is not None:
                        ts = max(t + o.cost, dma_free)
                        te = ts + o.nbytes / dma_bw
                        dma_free = te
                        o.finish = te + dma_lat
                        eng_free[e] = t + o.cost
                        busy[e] += o.cost
                    else:
                        o.finish = t + o.cost
                        eng_free[e] = o.finish
                        busy[e] += o.cost
                    order[e].append(i)
                    nsched += 1
                    progressed = True
                    for u in o.users:
                        remaining[u] -= 1
                        rt = o.finish + (sem_lat if ops[u].eng != e or o.dma_key is not None else 0.0)
                        if rt > ready_time[u]:
                            ready_time[u] = rt
                        if remaining[u] == 0:
                            heapq.heappush(future[ops[u].eng], (ready_time[u], u))
            if not progressed:
                nxt = float("inf")
                for e in ENGS:
                    if future[e] or avail[e]:
                        cand = eng_free[e] if avail[e] else float("inf")
                        if future[e]:
                            cand = min(cand, max(future[e][0][0], eng_free[e]))
                        if cand > t:
                            nxt = min(nxt, cand)
                        else:
                            nxt = min(nxt, t + 1.0)
                assert nxt < float("inf"), "scheduler stuck"
                t = nxt
        self.order = order
        makespan = max(o.finish for o in ops)
        self.makespan = makespan
        if verbose:
            print(f"[sched] ops={n} makespan={makespan/1000:.1f}us " +
                  " ".join(f"{e}:{busy[e]/1000:.0f}us/{len(order[e])}" for e in ENGS), flush=True)
        return makespan

    def emit(self, final_waits=()):
        nc = self.nc
        ops = self.ops
        order = self.order
        for e in ENGS:
            for p, i in enumerate(order[e]):
                ops[i].pos = p
        for o in ops:
            o.signal = False
        for o in ops:
            for d in o.deps:
                do = ops[d]
                if do.dma_key is not None:
                    continue
                if do.eng != o.eng:
                    do.signal = True
                elif o.eng != "pe":
                    do.signal = True
        for i in final_waits:
            if ops[i].dma_key is None:
                ops[i].signal = True
        for e in ENGS:
            c = 0
            for i in order[e]:
                o = ops[i]
                if o.dma_key is None and o.signal:
                    c += 1
                    o.semval = c
        dcount = {}
        dma_keys = []
        for o in ops:
            if o.dma_key is not None:
                if o.dma_key not in dcount:
                    dcount[o.dma_key] = 0
                    dma_keys.append(o.dma_key)
                dcount[o.dma_key] += 16
                o.semval = dcount[o.dma_key]
        sems = {}
        for e in ENGS:
            sems[e] = self.stack.enter_context(nc.semaphore("s_" + e))
        for k in dma_keys:
            sems["dma:" + k] = self.stack.enter_context(nc.semaphore("d_" + k))
        handles = {"pe": nc.tensor, "act": nc.scalar, "dve": nc.vector, "pool": nc.gpsimd, "sp": nc.sync}
        nwaits = 0

        def emit_engine(e, eh):
            nonlocal nwaits
            waited = {}
            for i in order[e]:
                o = ops[i]
                need = {}
                for d in o.deps:
                    do = ops[d]
                    if do.dma_key is not None:
                        k = "dma:" + do.dma_key
                        v = do.semval
                    else:
                        if do.eng == e and e == "pe":
                            continue
                        k = do.eng
                        v = do.semval
                    if v > need.get(k, 0):
                        need[k] = v
                for k, v in need.items():
                    if waited.get(k, 0) >= v:
                        continue
                    eh.wait_ge(sems[k], v)
                    waited[k] = v
                    nwaits += 1
                ins = o.fn(eh)
                if o.dma_key is not None:
                    ins.then_inc(sems["dma:" + o.dma_key], 16)
                elif o.signal:
                    ins.then_inc(sems[e], 1)
            if e == "sp":
                for i in final_waits:
                    o = ops[i]
                    k = ("dma:" + o.dma_key) if o.dma_key is not None else o.eng
                    if waited.get(k, 0) < o.semval:
                        eh.wait_ge(sems[k], o.semval)
                        waited[k] = o.semval

        with nc.Block() as block:
            @block.sync
            def _(eh):
                emit_engine("sp", eh)

            @block.scalar
            def _(eh):
                emit_engine("act", eh)

            @block.vector
            def _(eh):
                emit_engine("dve", eh)

            @block.gpsimd
            def _(eh):
                emit_engine("pool", eh)

            @block.tensor
            def _(eh):
                emit_engine("pe", eh)
        print(f"[emit] waits={nwaits} sems={len(sems)}", flush=True)

    def close(self):
        self.stack.close()


def c_mm(n, fp32=False):
    return max(64, n) / 2.12 * (4 if fp32 else 1) + 2


def c_act(n, psum=False):
    return ((172 if psum else 224) + n) / 1.2


def c_dve(n, mode=1, psum=False):
    if psum:
        return (120 + n) / 0.96
    return (150 + n / mode) / 0.96


def c_pool(n):
    return (200 + 2 * n) / 0.96


import numpy as np
from concourse.bass_utils import run_bass_kernel_spmd

SEQ = 2048
D = 1024
DIN = 2560
DFF = 4096
NT = SEQ // 128
NB = SEQ // 512
EPS = 1e-6
GELU_C = 0.7978845608028654
NS = 132

C_C = 0
C_ADAB = 8
C_N1G = 56
C_N2G = 64
C_LCW = 72
C_LCB = 88
C_GAB = 92
C_GXB = 96
C_AP = 100
C_LOG = 104
C_COG = 108
C_SCW = 112
C_FG = 124


class RR:
    def __init__(self, tiles):
        self.tiles = tiles
        self.i = 0

    def next(self):
        t = self.tiles[self.i % len(self.tiles)]
        self.i += 1
        return t


def build_program(verbose=True):
    nc = bass.Bass("TRN2", target_bir_lowering=False)
    S = Sched(nc)
    add = S.add

    def din(name, shape):
        return nc.dram_tensor(name, list(shape), F32, kind="ExternalInput").ap()

    x_d = din("x", [SEQ, D])
    small_d = din("small", [128, NS])
    adaw_d = din("ada_w", [D, 6 * D])
    win_d = din("w_in", [D, DIN])
    wout_d = din("w_out", [D, D])
    w1_d = din("w_mlp1", [D, DFF])
    w2_d = din("w_mlp2", [DFF, D])
    gaw_d = din("gaw", [512, 64])
    gxw_d = din("gxw", [512, 64])
    y_d = nc.dram_tensor("y", [SEQ, D], F32, kind="ExternalOutput").ap()

    ARENA_BYTES = 212480
    arena = S.sbuf("arena", [128, ARENA_BYTES // 4], F32)
    cur = [0]

    def carve_at(off, shape, dtype, name):
        n = 1
        for s in shape:
            n *= s
        nbytes = n * (4 if dtype == F32 else 2)
        assert off % 4 == 0 and nbytes % 4 == 0
        assert off + nbytes <= ARENA_BYTES, (name, off, nbytes)
        ap = arena[:, off // 4:(off + nbytes) // 4]
        if dtype != F32:
            ap = ap.bitcast(dtype)
        if len(shape) == 2:
            ap = ap.rearrange("p (a b) -> p a b", a=shape[0])
        elif len(shape) == 3:
            ap = ap.rearrange("p (a b c) -> p a b c", a=shape[0], b=shape[1])
        return ap, nbytes

    def carve(shape, dtype, name):
        ap, nbytes = carve_at(cur[0], shape, dtype, name)
        cur[0] += nbytes
        return ap

    def T(shape, dtype, name):
        return Tile(carve(shape, dtype, name), name)

    xbuf = [T([1024], F32, f"x{i}") for i in range(8)]
    xn = [T([1024], F32, f"xn{i}") for i in range(2)]
    stg_off = [cur[0], cur[0] + 8192]
    stg = [T([2048], F32, f"stg{i}") for i in range(2)]
    stg_bf = [carve_at(stg_off[i], [8, 512], BF16, "stgbf")[0] for i in range(2)]
    stg_f3 = [carve_at(stg_off[i], [2, 1024], F32, "stgf3")[0] for i in range(2)]
    wout_full = carve([8, 1024], BF16, "wout")
    Wout = [Tile(wout_full[:, k, :], f"wout{k}") for k in range(8)]
    gate_bc = T([1024], F32, "gate_bc")
    ident = T([128], F32, "ident")
    ones = T([128], F32, "ones")
    Wa = [T([128], BF16, f"wa{c}") for c in range(4)]
    Wx = [T([128], BF16, f"wx{c}") for c in range(4)]
    small = T([NS], F32, "small")
    mc = [T([8], F32, f"mc{i}") for i in range(6)]
    gs1 = T([8], F32, "gs1")
    gs2 = T([8], F32, "gs2")
    sc_th = T([8], F32, "sc_th")
    sc_f = T([8], F32, "sc_f")
    sc_b = T([8], BF16, "sc_b")
    spt = [T([4], F32, f"spt{i}") for i in range(4)]
    cp = T([4], F32, "cp")
    chalf = T([4], F32, "chalf")
    hgab = T([4], F32, "hgab")
    hgxb = T([4], F32, "hgxb")
    ss = [T([1], F32, f"ss{i}") for i in range(4)]
    rstd = [T([1], F32, f"rstd{i}") for i in range(4)]
    hcarry = [T([1], F32, f"hc{c}") for c in range(4)]
    haloX = [T([3], F32, f"hx{c}") for c in range(4)]
    chalo = [T([2], F32, f"ch{c}") for c in range(4)]
    eps4 = T([4, 16], F32, "eps4")
    mhalf = T([4, 16], F32, "mhalf")
    mhalf1 = T([1], F32, "mhalf1")
    mask2 = T([2], BF16, "mask2")
    Efull = carve([8, 128], BF16, "E")
    E = Tile(Efull, "E")
    pidx = T([1], F32, "pidx")
    iot = T([128], F32, "iot")
    dg = [T([128], F32, f"dg{i}") for i in range(2)]
    gst = [T([128], F32, f"gst{i}") for i in range(2)]

    win_full = carve([8, DIN], BF16, "win")
    Win = [Tile(win_full[:, :, g * 256:(g + 1) * 256], f"win{g}") for g in range(10)]
    PERM_END = cur[0]

    MIXR = cur[0]
    hT = []
    hT_off = []
    for i in range(2):
        hT_off.append(cur[0])
        apf = carve([8, 512], BF16, f"hT{i}")
        hT.append([Tile(apf[:, k, :], f"hT{i}_{k}") for k in range(8)])
    G2_OFF = cur[0]
    apf = carve([8, 512], BF16, "yT")
    yT = [Tile(apf[:, k, :], f"yT_{k}") for k in range(8)]
    pX = RR([T([516], F32, f"X{i}") for i in range(2)])
    pXL = RR([T([512], F32, f"XL{i}") for i in range(2)])
    pR = RR([T([512], F32, f"R{i}") for i in range(2)])
    pA = RR([T([512], F32, f"A{i}") for i in range(2)])
    G4_OFF = cur[0]
    pI = RR([T([512], F32, f"I{i}") for i in range(2)])
    pUL = RR([T([512], F32, f"UL{i}") for i in range(2)])
    pSQ = RR([T([512], F32, f"SQ{i}") for i in range(2)])
    pY = RR([T([512], F32, f"Y{i}") for i in range(10)])
    pXLB = RR([T([512], BF16, f"XLB{i}") for i in range(2)])
    pYSQ = RR([T([512], BF16, f"YSQ{i}") for i in range(2)])
    RS = T([4, 128], F32, "RS")
    RHI = T([512], BF16, "RHI")
    RLO = T([512], BF16, "RLO")
    MIXR_END = cur[0]
    mixG = [hT[0], hT[1], yT,
            pX.tiles + pXL.tiles + pR.tiles + pA.tiles,
            pI.tiles + pUL.tiles + pSQ.tiles + pY.tiles + pXLB.tiles + pYSQ.tiles + [RS, RHI, RLO]]
    W1e = [None] * 3
    W2e = [None] * 3
    h2T_b0, nb = carve_at(hT_off[0], [8, 512], BF16, "h2Tb0")
    ap_, nb = carve_at(hT_off[1], [8, 512], BF16, "w1e0")
    W1e[0] = Tile(ap_, "w1e0")
    o = G2_OFF
    h1T = []
    for i in range(2):
        ap_, nb = carve_at(o, [4, 512], BF16, f"h1T{i}"); o += nb
        h1T.append([Tile(ap_[:, j, :], f"h1T{i}_{j}") for j in range(4)])
    h2T_b1, nb = carve_at(o, [8, 512], BF16, "h2Tb1"); o += nb
    h2T = [[Tile(h2T_b0[:, k, :], f"h2T{k}_0"), Tile(h2T_b1[:, k, :], f"h2T{k}_1")] for k in range(8)]
    ap_, nb = carve_at(o, [4, 1024], BF16, "w2e0"); o += nb
    W2e[0] = [Tile(ap_[:, kk, :], f"w2e0_{kk}") for kk in range(4)]
    assert o <= G4_OFF, (o, G4_OFF)
    o = G4_OFF
    for r in (1, 2):
        ap_, nb = carve_at(o, [8, 512], BF16, f"w1e{r}"); o += nb
        W1e[r] = Tile(ap_, f"w1e{r}")
        ap_, nb = carve_at(o, [4, 1024], BF16, f"w2e{r}"); o += nb
        W2e[r] = [Tile(ap_[:, kk, :], f"w2e{r}_{kk}") for kk in range(4)]
    pRl_tiles = []
    for i in range(3):
        ap_, nb = carve_at(o, [512], F32, f"Rl{i}"); o += nb
        pRl_tiles.append(Tile(ap_, f"Rl{i}"))
    pRl = RR(pRl_tiles)
    assert o <= MIXR_END, (o, MIXR_END)
    mlpG = [[h2T[k][0] for k in range(8)], [W1e[0]], [t for hh in h1T for t in hh],
            [h2T[k][1] for k in range(8)] + W2e[0],
            [W1e[1], W1e[2]] + W2e[1] + W2e[2] + pRl_tiles]
    WINR = WINR_END = 0
    if verbose:
        print(f"[sbuf] perm={PERM_END} winr={WINR_END-WINR} mixr={MIXR_END-MIXR} total={cur[0]} / {ARENA_BYTES}", flush=True)
    assert cur[0] <= ARENA_BYTES

    def alias(new_tiles, old_tiles):
        acc = set()
        for t in old_tiles:
            acc.update(t.readers)
            if t.last_w is not None:
                acc.add(t.last_w)
        for t in new_tiles:
            a2 = set(acc)
            a2.update(t.readers)
            if t.last_w is not None:
                a2.add(t.last_w)
            t.readers = list(a2)
            t.last_w = None

    banks = [S.psum(f"pb{i}", [128, 512], F32) for i in range(8)]
    mm = RR([Tile(banks[i][:, :], f"mm{i}", excl=True) for i in range(7)])
    tr = mm
    psS_full = banks[7][:, 0:64].rearrange("p (a b) -> p a b", a=4)
    psS = Tile(psS_full, "psS", excl=True)

    MULT, ADD, SUB = ALU.mult, ALU.add, ALU.subtract

    S.dma(small.ap, small_d, writes=[small], key="small", nbytes=128 * NS * 4)

    win_done = []
    ada_first = []

    def load_x(t, after=()):
        xt = xbuf[t % 8]
        S.dma(xt.ap, x_d[t * 128:(t + 1) * 128, :], writes=[xt], key=f"x{t % 8}", nbytes=128 * 1024 * 4, name=f"ldx{t}", after=after)

    for t in range(4):
        load_x(t)

    add("pool", lambda e: e.iota(iot.ap, [[1, 128]], 0, channel_multiplier=0, allow_small_or_imprecise_dtypes=True), [], [iot], c_pool(128))
    add("pool", lambda e: e.iota(pidx.ap, [[0, 1]], 0, channel_multiplier=1, allow_small_or_imprecise_dtypes=True), [], [pidx], c_pool(1))
    add("dve", lambda e: e.tensor_scalar(ident.ap, iot.ap, pidx.ap, None, ALU.is_equal), [iot, pidx], [ident], c_dve(128))
    add("pool", lambda e: e.memset(ones.ap, 1.0), [], [ones], c_pool(128))
    add("pool", lambda e: e.memset(mhalf.ap, -0.5), [], [mhalf], c_pool(64))
    add("pool", lambda e: e.memset(mhalf1.ap, -0.5), [], [mhalf1], c_pool(1))
    add("pool", lambda e: e.memset(eps4[:, :, 0:8], 4.0 * EPS), [], [eps4], c_pool(32))
    add("pool", lambda e: e.memset(eps4[:, :, 8:16], EPS), [eps4], [eps4], c_pool(32))
    add("pool", lambda e: e.memset(mask2.ap, 0.0), [], [mask2], c_pool(2))
    add("pool", lambda e: e.memset(mask2[0:64, 0:1], 1.0 / 64), [mask2], [mask2], c_pool(1))
    add("pool", lambda e: e.memset(mask2[64:128, 1:2], 1.0 / 64), [mask2], [mask2], c_pool(1))
    for c in range(4):
        add("pool", lambda e, c=c: e.memset(hcarry[c].ap, 0.0), [], [hcarry[c]], c_pool(1))
        add("pool", lambda e, c=c: e.memset(haloX[c].ap, 0.0), [], [haloX[c]], c_pool(3))
        add("pool", lambda e, c=c: e.memset(chalo[c].ap, 0.0), [], [chalo[c]], c_pool(2))
    ef = stg[1]
    ef_v = ef.ap.rearrange("p (a b c) -> p a b c", a=8, b=2)[:, :, :, 0:64]
    add("pool", lambda e: e.iota(ef_v, [[-2, 8], [-1, 2], [0, 64]], 0, channel_multiplier=1,
                                 allow_small_or_imprecise_dtypes=True), [], [ef], c_pool(1024))
    add("dve", lambda e: e.tensor_single_scalar(E.ap.rearrange("p a (b c) -> p a b c", b=2), ef_v, 0.0, ALU.is_equal),
        [ef], [E], c_dve(1024))

    ccol = small[:, C_C:C_C + 8]
    add("act", lambda e: e.activation(sc_th.ap, ccol, AF.Tanh, scale=0.5), [small], [sc_th], c_act(8)).group = "e"
    add("dve", lambda e: e.tensor_scalar(sc_th.ap, sc_th.ap, 0.5, 0.5, MULT, ADD), [sc_th], [sc_th], c_dve(8))
    add("dve", lambda e: e.tensor_tensor(sc_f.ap, sc_th.ap, ccol, MULT), [sc_th, small], [sc_f], c_dve(8))
    add("dve", lambda e: e.tensor_copy(sc_b.ap, sc_f.ap), [sc_f], [sc_b], c_dve(8))

    apc = small[:, C_AP:C_AP + 4]
    add("act", lambda e: e.activation(spt[0].ap, apc, AF.Abs), [small], [spt[0]], c_act(4))
    add("act", lambda e: e.activation(spt[1].ap, spt[0].ap, AF.Exp, scale=-1.0), [spt[0]], [spt[1]], c_act(4)).group = "e"
    add("act", lambda e: e.activation(spt[2].ap, spt[1].ap, AF.Ln, bias=1.0), [spt[1]], [spt[2]], c_act(4)).group = "l"
    add("dve", lambda e: e.tensor_scalar_max(spt[3].ap, apc, 0.0), [small], [spt[3]], c_dve(4))
    add("dve", lambda e: e.tensor_tensor(spt[3].ap, spt[3].ap, spt[2].ap, ADD), [spt[3], spt[2]], [spt[3]], c_dve(4))
    add("dve", lambda e: e.tensor_scalar(cp.ap, spt[3].ap, -8.0, None, MULT), [spt[3]], [cp], c_dve(4))
    add("dve", lambda e: e.tensor_scalar(chalf.ap, spt[3].ap, -4.0, None, MULT), [spt[3]], [chalf], c_dve(4))
    add("dve", lambda e: e.tensor_scalar(hgab.ap, small[:, C_GAB:C_GAB + 4], 0.5, None, MULT), [small], [hgab], c_dve(4))
    add("dve", lambda e: e.tensor_scalar(hgxb.ap, small[:, C_GXB:C_GXB + 4], 0.5, None, MULT), [small], [hgxb], c_dve(4))

    stg_rr = RR(stg)

    def ada_piece(p):
        st = stg_rr.next()
        si = stg.index(st)
        src = adaw_d[:, p * 512:(p + 1) * 512].rearrange("(k q) c -> q k c", q=128)
        o_ = S.dma(stg_bf[si], src, writes=[st], key=f"stg{si}", nbytes=2 * 1024 * 1024, eng="pool", name=f"ada{p}",
                   after=(win_done if p >= 4 else ()))
        if p < 4:
            ada_first.append(o_.id)
        ps = mm.next()
        for jj in range(4):
            for k in range(8):
                add("pe", lambda e, si=si, jj=jj, k=k, ps=ps: e.matmul(
                    ps[:, jj:jj + 1], stg_bf[si][:, k, jj * 128:(jj + 1) * 128], sc_b[:, k:k + 1],
                    start=(k == 0), stop=(k == 7)), [st, sc_b], [ps], c_mm(64) + 30)
        j0 = p * 4
        mt, mo = mc[p // 2], (p % 2) * 4
        add("dve", lambda e, ps=ps, j0=j0, mt=mt, mo=mo: e.tensor_tensor(mt[:, mo:mo + 4], ps[:, 0:4],
                                                                       small[:, C_ADAB + j0:C_ADAB + j0 + 4], ADD),
            [ps, small] + ([mt] if mo else []), [mt], c_dve(4, psum=True))

    def make_gs(gs, sct, g_c0):
        add("dve", lambda e: e.scalar_tensor_tensor(gs.ap, sct.ap, 1.0,
                                                    small[:, g_c0:g_c0 + 8], ADD, MULT), [sct, small], [gs], c_dve(8))

    def load_bc(col_tile, c0):
        for half in range(2):
            ps = mm.next()
            for kk in range(4):
                k = half * 4 + kk
                d = dg[k % 2]
                add("dve", lambda e, d=d, k=k: e.tensor_scalar(d.ap, ident.ap, col_tile[:, c0 + k:c0 + k + 1], None, MULT),
                    [ident, col_tile], [d], c_dve(128))
                add("pe", lambda e, d=d, kk=kk, ps=ps: e.matmul(ps[:, kk * 128:(kk + 1) * 128], ones.ap, d.ap,
                                                               start=True, stop=True), [ones, d], [ps], c_mm(128, fp32=True))
            add("act", lambda e, ps=ps, half=half: e.activation(gate_bc[:, half * 512:(half + 1) * 512], ps.ap, AF.Copy),
                [ps], [gate_bc], c_act(512, psum=True))

    for p in range(4):
        ada_piece(p)
    make_gs(gs1, mc[1], C_N1G)

    def load_win():
        for g in (0, 2, 8, 6, 4, 1, 3, 9, 7, 5):
            src = win_d[:, g * 256:(g + 1) * 256].rearrange("(k q) c -> q k c", q=128)
            o_ = S.dma(Win[g].ap, src, writes=[Win[g]], key=f"win{g}", nbytes=1024 * 1024, eng="pool", name=f"win{g}",
                       after=(ada_first[:2] if g in (0, 2) else ada_first))
            win_done.append(o_.id)

    load_win()
    for t in range(4, 8):
        load_x(t, after=win_done)

    gst_rr = RR(gst)
    for c in range(4):
        for (wd, Wt, nm) in ((gaw_d, Wa, "a"), (gxw_d, Wx, "x")):
            g = gst_rr.next()
            gi = gst.index(g)
            add("pool", lambda e, g=g: e.memset(g.ap, 0.0), [], [g], c_pool(128))
            S.dma(g[0:64, 0:64], wd[(2 * c) * 64:(2 * c + 1) * 64, :], writes=[g], key=f"gst{gi}", nbytes=16384)
            S.dma(g[64:128, 64:128], wd[(2 * c + 1) * 64:(2 * c + 2) * 64, :], writes=[g], key=f"gst{gi}", nbytes=16384)
            add("pool", lambda e, g=g, Wt=Wt, c=c: e.tensor_copy(Wt[c].ap, g.ap), [g], [Wt[c]], c_pool(128))

    def load_wout():
        for i in range(4):
            st = stg_rr.next()
            si = stg.index(st)
            src = wout_d[i * 256:(i + 1) * 256, :].rearrange("(k q) c -> q k c", q=128)
            S.dma(stg_f3[si], src, writes=[st], key=f"stg{si}", nbytes=1024 * 1024, name=f"wout{i}", after=win_done)
            for j in range(2):
                k = i * 2 + j
                add("pool", lambda e, si=si, j=j, k=k: e.tensor_tensor(Wout[k].ap, stg_f3[si][:, j, :], gate_bc.ap, MULT),
                    [st, gate_bc], [Wout[k]], c_pool(1024))


    def rmsnorm_T(tiles, gs, sht, dst, ncol0):
        for pair in range(2):
            tl = tiles[pair * 2:pair * 2 + 2]
            for i, xt in enumerate(tl):
                xs = xn[i]
                add("act", lambda e, xs=xs, xt=xt, i=i: e.activation(xs.ap, xt.ap, AF.Square, accum_out=ss[i].ap),
                    [xt], [xs, ss[i]], c_act(1024) + 80)
                add("pool", lambda e, i=i: e.tensor_scalar(rstd[i].ap, ss[i].ap, 1.0 / D, EPS, MULT, ADD), [ss[i]], [rstd[i]], c_pool(1))
                add("pool", lambda e, i=i: e.tensor_tensor(rstd[i].ap, rstd[i].ap, mhalf1.ap, ALU.pow), [rstd[i], mhalf1], [rstd[i]], 1500)
                add("act", lambda e, xs=xs, xt=xt, i=i: e.activation(xs.ap, xt.ap, AF.Copy, scale=rstd[i].ap),
                    [xt, rstd[i]], [xs], c_act(1024) + 80)
            for q in range(4):
                bank = tr.next()
                for kk in range(2):
                    k = 2 * q + kk
                    for i in range(2):
                        add("pe", lambda e, bank=bank, kk=kk, i=i, k=k: e.transpose(
                            bank[:, (kk * 2 + i) * 128:(kk * 2 + i + 1) * 128], xn[i][:, k * 128:(k + 1) * 128], ident.ap),
                            [xn[i], ident], [bank], c_mm(128) + 20)
                for kk in range(2):
                    k = 2 * q + kk
                    dt_ = dst[k]
                    oap = dt_[:, ncol0 + pair * 256: ncol0 + (pair + 1) * 256] if ncol0 is not None else None
                    if q == 0:
                        add("act", lambda e, bank=bank, kk=kk, k=k, dt_=dt_, pair=pair: e.activation(
                            dt_[:, pair * 256:(pair + 1) * 256], bank[:, kk * 256:(kk + 1) * 256], AF.Identity,
                            bias=sht[:, k:k + 1], scale=gs[:, k:k + 1]),
                            [bank, sht, gs], [dt_], c_act(256, psum=True) + 150)
                    else:
                        add("dve", lambda e, bank=bank, kk=kk, k=k, dt_=dt_, pair=pair: e.tensor_scalar(
                            dt_[:, pair * 256:(pair + 1) * 256], bank[:, kk * 256:(kk + 1) * 256],
                            gs[:, k:k + 1], sht[:, k:k + 1], MULT, ADD),
                            [bank, sht, gs], [dt_], c_dve(256, psum=True))

    def inproj(hTb, m):
        ps = mm.next()
        g, off = m // 2, (m % 2) * 128
        for k in range(8):
            add("pe", lambda e, ps=ps, g=g, off=off, k=k: e.matmul(ps.ap, Win[g][:, k, off:off + 128], hTb[k].ap,
                                                                   start=(k == 0), stop=(k == 7)),
                [Win[g], hTb[k]], [ps], c_mm(512))
        return ps

    def act(fn, reads, writes, cost, group=None):
        o_ = add("act", fn, reads, writes, cost)
        o_.group = group
        return o_

    def lru_A(gb, hTb, c):
        ps_lx = inproj(hTb, c)
        X = pX.next()
        act(lambda e: e.activation(X[:, 3:515], ps_lx.ap, AF.Copy), [ps_lx], [X], c_act(512, True))
        add("dve", lambda e: e.tensor_copy(X[:, 0:3], haloX[c].ap), [haloX[c], X], [X], c_dve(3))
        add("dve", lambda e: e.tensor_copy(haloX[c].ap, X[:, 512:515]), [X], [haloX[c]], c_dve(3))
        ps_ly = inproj(hTb, 4 + c)
        UL = pUL.next()
        SQ = pSQ.next()
        act(lambda e: e.activation(UL.ap, ps_ly.ap, AF.Copy), [ps_ly], [UL], c_act(512, True))
        act(lambda e: e.activation(SQ.ap, ps_ly.ap, AF.Square, scale=0.044715 ** 0.5), [ps_ly], [SQ], c_act(512, True))
        XL = pXL.next()
        w = lambda k: small[:, C_LCW + c * 4 + k:C_LCW + c * 4 + k + 1]
        add("dve", lambda e: e.tensor_scalar(XL.ap, X[:, 3:515], w(3), small[:, C_LCB + c:C_LCB + c + 1], MULT, ADD),
            [X, small], [XL], c_dve(512, 2))
        for k in (2, 1, 0):
            add("dve", lambda e, k=k: e.scalar_tensor_tensor(XL.ap, X[:, k:k + 512], w(k), XL.ap, MULT, ADD),
                [X, small, XL], [XL], c_dve(512))
        XLB = pXLB.next()
        add("dve", lambda e: e.tensor_copy(XLB.ap, XL.ap), [XL], [XLB], c_dve(512, 2))
        ps_a = mm.next()
        add("pe", lambda e: e.matmul(ps_a.ap, Wa[c].ap, XLB.ap, start=True, stop=True), [Wa[c], XLB], [ps_a], c_mm(512))
        ps_x = mm.next()
        add("pe", lambda e: e.matmul(ps_x.ap, Wx[c].ap, XLB.ap, start=True, stop=True), [Wx[c], XLB], [ps_x], c_mm(512))
        R = pR.next()
        I = pI.next()
        act(lambda e: e.activation(R.ap, ps_a.ap, AF.Tanh, bias=hgab[:, c:c + 1], scale=0.5), [ps_a, hgab], [R], c_act(512, True) + 80, "e")
        act(lambda e: e.activation(I.ap, ps_x.ap, AF.Tanh, bias=hgxb[:, c:c + 1], scale=0.5), [ps_x, hgxb], [I], c_act(512, True) + 80, "e")
        return dict(c=c, gb=gb, XL=XL, UL=UL, SQ=SQ, R=R, I=I)

    def lru_B(ctx, Ys):
        c, gb, XL, UL, SQ, R, I = (ctx[k] for k in ("c", "gb", "XL", "UL", "SQ", "R", "I"))
        A = pA.next()
        act(lambda e: e.activation(A.ap, R.ap, AF.Exp, bias=chalf[:, c:c + 1], scale=chalf[:, c:c + 1]), [R, chalf], [A], c_act(512) + 160, "e")
        act(lambda e: e.activation(R.ap, R.ap, AF.Exp, bias=cp[:, c:c + 1], scale=cp[:, c:c + 1]), [R, cp], [R], c_act(512) + 160, "e")
        act(lambda e: e.activation(R.ap, R.ap, AF.Sqrt, bias=0.25 + 2e-6, scale=-0.25), [R], [R], c_act(512), "s")
        if gb == 0:
            add("pool", lambda e: e.memset(R[:, 0:1], 0.5), [R], [R], c_pool(1))
        add("dve", lambda e: e.scalar_tensor_tensor(I.ap, I.ap, 1.0, XL.ap, ADD, MULT), [I, XL], [I], c_dve(512))
        add("dve", lambda e: e.tensor_tensor(I.ap, R.ap, I.ap, MULT), [R, I], [I], c_dve(512))
        H = XL
        add("dve", lambda e: e.tensor_tensor_scan(H.ap, A.ap, I.ap, hcarry[c].ap, MULT, ADD),
            [A, I, hcarry[c]], [H], (150 + 1024) / 0.96)
        add("dve", lambda e: e.tensor_copy(hcarry[c].ap, H[:, 511:512]), [H], [hcarry[c]], c_dve(1))
        add("dve", lambda e: e.scalar_tensor_tensor(SQ.ap, SQ.ap, 1.0, UL.ap, ADD, MULT), [SQ, UL], [SQ], c_dve(512))
        act(lambda e: e.activation(SQ.ap, SQ.ap, AF.Tanh, scale=GELU_C), [SQ], [SQ], c_act(512), "e")
        add("dve", lambda e: e.scalar_tensor_tensor(SQ.ap, SQ.ap, 1.0, UL.ap, ADD, MULT), [SQ, UL], [SQ], c_dve(512))
        Y = pY.next()
        add("dve", lambda e: e.tensor_tensor(Y.ap, SQ.ap, H.ap, MULT), [SQ, H], [Y], c_dve(512))
        Ys[c] = Y
        stats(Y, c)

    def stats(Y, hc):
        YSQ = pYSQ.next()
        act(lambda e: e.activation(YSQ.ap, Y.ap, AF.Square), [Y], [YSQ], c_act(512))
        for t in range(4):
            add("pe", lambda e, t=t: e.matmul(psS[:, t, 2 * hc:2 * hc + 2], YSQ[:, t * 128:(t + 1) * 128], mask2.ap,
                                              start=True, stop=True), [YSQ, mask2], [psS], c_mm(64) + 40)

    def conv_chunk(gb, hTb, c, Ys):
        ps_v = inproj(hTb, 16 + c)
        UV = pA.next()
        act(lambda e: e.activation(UV.ap, ps_v.ap, AF.Copy), [ps_v], [UV], c_act(512, True))
        ps_c = inproj(hTb, 12 + c)
        CV = pX.next()
        add("dve", lambda e: e.tensor_tensor(CV[:, 2:514], ps_c.ap, UV.ap, MULT), [ps_c, UV], [CV], c_dve(512, psum=True))
        add("dve", lambda e: e.tensor_copy(CV[:, 0:2], chalo[c].ap), [chalo[c], CV], [CV], c_dve(2))
        add("dve", lambda e: e.tensor_copy(chalo[c].ap, CV[:, 512:514]), [CV], [chalo[c]], c_dve(2))
        ps_b = inproj(hTb, 8 + c)
        CO = UV
        w = lambda k: small[:, C_SCW + c * 3 + k:C_SCW + c * 3 + k + 1]
        add("dve", lambda e: e.tensor_scalar(CO.ap, CV[:, 2:514], w(2), None, MULT), [CV, small], [CO], c_dve(512, 2))
        for k in (1, 0):
            add("dve", lambda e, k=k: e.scalar_tensor_tensor(CO.ap, CV[:, k:k + 512], w(k), CO.ap, MULT, ADD),
                [CV, small, CO], [CO], c_dve(512))
        Y = pY.next()
        add("dve", lambda e: e.tensor_tensor(Y.ap, ps_b.ap, CO.ap, MULT), [ps_b, CO], [Y], c_dve(512, psum=True))
        Ys[4 + c] = Y
        stats(Y, 4 + c)

    def norm_apply(Ys):
        add("dve", lambda e: e.tensor_tensor(RS[:, :, 0:16], psS.ap, eps4.ap, ADD), [psS, eps4, RS], [RS], c_dve(64, psum=True))
        act(lambda e: e.activation(RS[:, :, 0:16], RS[:, :, 0:16], AF.Sqrt), [RS], [RS], c_act(64), "s")
        add("dve", lambda e: e.reciprocal(RS[:, :, 0:16], RS[:, :, 0:16]), [RS], [RS], (150 + 8 * 64) / 0.96)
        bank = tr.next()
        for t in range(4):
            add("pe", lambda e, t=t: e.transpose(bank[:, t * 128:(t + 1) * 128], RS[:, t, :], ident.ap), [RS, ident], [bank], c_mm(128) + 20)
        act(lambda e: e.activation(RHI.ap, bank.ap, AF.Copy), [bank], [RHI], c_act(512, True))
        add("dve", lambda e: e.tensor_tensor(RLO.ap, bank.ap, RHI.ap, SUB), [bank, RHI], [RLO], c_dve(512, psum=True))
        for c8 in range(8):
            ps = mm.next()
            add("pe", lambda e, ps=ps, c8=c8: e.matmul(ps.ap, E[:, c8, :], RHI.ap, start=True, stop=False), [E, RHI], [ps], c_mm(512))
            add("pe", lambda e, ps=ps, c8=c8: e.matmul(ps.ap, E[:, c8, :], RLO.ap, start=False, stop=True), [E, RLO], [ps], c_mm(512))
            gcol = (C_LOG + c8) if c8 < 4 else (C_COG + c8 - 4)
            Y = Ys[c8]
            add("dve", lambda e, ps=ps, Y=Y, gcol=gcol, c8=c8: e.scalar_tensor_tensor(
                yT[c8].ap, Y.ap, small[:, gcol:gcol + 1], ps.ap, MULT, MULT), [Y, small, ps], [yT[c8]], c_dve(512, psum=True))

    def outproj(tiles):
        for xt in tiles:
            ti = tiles.index(xt)
            for nh in range(2):
                ps = mm.next()
                for k in range(8):
                    add("pe", lambda e, ps=ps, k=k, ti=ti, nh=nh: e.matmul(
                        ps.ap, yT[k][:, ti * 128:(ti + 1) * 128], Wout[k][:, nh * 512:(nh + 1) * 512],
                        start=(k == 0), stop=(k == 7)), [yT[k], Wout[k]], [ps], c_mm(512))
                add("dve", lambda e, ps=ps, xt=xt, nh=nh: e.tensor_tensor(
                    xt[:, nh * 512:(nh + 1) * 512], xt[:, nh * 512:(nh + 1) * 512], ps.ap, ADD), [xt, ps], [xt], c_dve(512, psum=True))

    def final_tiles(half, i8s):
        for i8 in i8s:
            t = half * 8 + i8
            xt = xbuf[t % 8]
            i = i8 % 4
            xs = xn[i % 2]
            add("act", lambda e, xs=xs, xt=xt, i=i: e.activation(xs.ap, xt.ap, AF.Square, accum_out=ss[i].ap),
                [xt], [xs, ss[i]], c_act(1024) + 80)
            add("pool", lambda e, i=i: e.tensor_scalar(rstd[i].ap, ss[i].ap, 1.0 / D, EPS, MULT, ADD), [ss[i]], [rstd[i]], c_pool(1))
            add("pool", lambda e, i=i: e.tensor_tensor(rstd[i].ap, rstd[i].ap, mhalf1.ap, ALU.pow), [rstd[i], mhalf1], [rstd[i]], 1500)
            add("dve", lambda e, xt=xt, i=i: e.scalar_tensor_tensor(xt.ap, xt.ap, rstd[i].ap, gate_bc.ap, MULT, MULT),
                [xt, rstd[i], gate_bc], [xt], c_dve(1024))
            st_ = S.dma(y_d[t * 128:(t + 1) * 128, :], xt.ap, reads=[xt], key=f"x{t % 8}", nbytes=128 * 1024 * 4, name=f"st{t}")
            stores.append(st_.id)
            if half == 0:
                load_x(t + 8)

    w1_rr = [0]
    h1_rr = [0]
    stores = []
    if verbose:
        print("[mark] prologue end", len(S.ops))
    for half in range(2):
        if verbose:
            print("[mark] half", half, "start", len(S.ops))
        if half == 1:
            for gi in range(len(mixG)):
                alias(mixG[gi], mlpG[gi])
        w1_rr[0] = 0
        add("pool", lambda e: e.memset(RS.ap, 0.0), [], [RS], c_pool(512))
        blocks = []
        for bi in range(2):
            gb = half * 2 + bi
            blocks.append(dict(gb=gb, tiles=[xbuf[(gb * 4 + i) % 8] for i in range(4)], hTb=hT[gb % 2],
                               Ys=[None] * 8, ctxs=[None] * 4))

        def start_block(B):
            rmsnorm_T(B["tiles"], gs1, mc[0], B["hTb"], None)
            B["ctxs"][0] = lru_A(B["gb"], B["hTb"], 0)
            B["ctxs"][1] = lru_A(B["gb"], B["hTb"], 1)

        def chunk_loop(B, after_c1=None):
            gb, hTb, Ys, ctxs = B["gb"], B["hTb"], B["Ys"], B["ctxs"]
            for c in range(4):
                lru_B(ctxs[c], Ys)
                conv_chunk(gb, hTb, c, Ys)
                if c + 2 < 4:
                    ctxs[c + 2] = lru_A(gb, hTb, c + 2)
                if c == 1 and after_c1 is not None:
                    after_c1()
                if gb == 1:
                    ada_piece(6 + c)
                    if c >= 2:
                        ada_piece(8 + c)

        B0, B1 = blocks
        if verbose and half == 0:
            print("[mark] gb0 norm start", len(S.ops))
        start_block(B0)
        chunk_loop(B0)
        if half == 0:
            for p in (4, 5):
                ada_piece(p)
            load_bc(mc[2], 0)
        start_block(B1)
        if half == 0:
            load_wout()
        if verbose and half == 0:
            print("[mark] gb0 norm_apply", len(S.ops))
        norm_apply(B0["Ys"])
        if verbose and half == 0:
            print("[mark] gb0 end", len(S.ops))
        chunk_loop(B1, after_c1=lambda: outproj(B0["tiles"]))
        if half == 0:
            make_gs(gs2, mc[4], C_N2G)
            load_bc(mc[5], 0)
        alias(mlpG[0], mixG[0])
        rmsnorm_T(B0["tiles"], gs2, mc[3], [h2T[k][0] for k in range(8)], None)
        norm_apply(B1["Ys"])
        outproj(B1["tiles"])
        tiles_prev = B1["tiles"]
        if verbose:
            print("[mark] half", half, "mlp start", len(S.ops))
        for gi in range(1, len(mixG)):
            alias(mlpG[gi], mixG[gi])
        rmsnorm_T(tiles_prev, gs2, mc[3], [h2T[k][1] for k in range(8)], None)
        for e8 in range(8):
            r = w1_rr[0] % 3
            w1_rr[0] += 1
            src = w1_d[:, e8 * 512:(e8 + 1) * 512].rearrange("(k q) c -> q k c", q=128)
            S.dma(W1e[r].ap, src, writes=[W1e[r]], key=f"w1e{r}", nbytes=2 * 1024 * 1024, eng="pool", name=f"w1e{e8}")
            for piece in range(2):
                st = stg_rr.next()
                si = stg.index(st)
                src = w2_d[e8 * 512 + piece * 256:e8 * 512 + (piece + 1) * 256, :].rearrange("(k q) c -> q k c", q=128)
                S.dma(stg_f3[si], src, writes=[st], key=f"stg{si}", nbytes=1024 * 1024, name=f"w2e{e8}_{piece}")
                for j in range(2):
                    kk = piece * 2 + j
                    add("pool", lambda e, si=si, j=j, r=r, kk=kk: e.tensor_tensor(W2e[r][kk].ap, stg_f3[si][:, j, :], gate_bc.ap, MULT),
                        [st, gate_bc], [W2e[r][kk]], c_pool(1024))
            if e8 == 7:
                load_bc(small, C_FG)
            for bi in range(2):
                gb = half * 2 + bi
                tiles = [xbuf[(gb * 4 + i) % 8] for i in range(4)]
                hb = h1T[h1_rr[0] % 2]
                h1_rr[0] += 1
                for j in range(4):
                    ps = mm.next()
                    for k in range(8):
                        add("pe", lambda e, ps=ps, r=r, k=k, j=j, bi=bi: e.matmul(
                            ps.ap, W1e[r][:, k, j * 128:(j + 1) * 128], h2T[k][bi].ap, start=(k == 0), stop=(k == 7)),
                            [W1e[r], h2T[k][bi]], [ps], c_mm(512))
                    Rl = pRl.next()
                    act(lambda e, ps=ps, Rl=Rl: e.activation(Rl.ap, ps.ap, AF.Relu), [ps], [Rl], c_act(512, True))
                    add("dve", lambda e, ps=ps, Rl=Rl, hb=hb, j=j: e.tensor_tensor(hb[j].ap, ps.ap, Rl.ap, MULT),
                        [ps, Rl], [hb[j]], c_dve(512, psum=True))
                for ti, xt in enumerate(tiles):
                    for nh in range(2):
                        ps = mm.next()
                        for kk in range(4):
                            add("pe", lambda e, ps=ps, kk=kk, ti=ti, nh=nh, hb=hb, r=r: e.matmul(
                                ps.ap, hb[kk][:, ti * 128:(ti + 1) * 128], W2e[r][kk][:, nh * 512:(nh + 1) * 512],
                                start=(kk == 0), stop=(kk == 3)), [hb[kk], W2e[r][kk]], [ps], c_mm(512))
                        add("dve", lambda e, ps=ps, xt=xt, nh=nh: e.tensor_tensor(
                            xt[:, nh * 512:(nh + 1) * 512], xt[:, nh * 512:(nh + 1) * 512], ps.ap, ADD), [xt, ps], [xt], c_dve(512, psum=True))
                if e8 == 7:
                    final_tiles(half, range(bi * 4, bi * 4 + 4))
        if half == 0:
            load_bc(mc[5], 0)

    stores = [i for i in stores if i >= 0]
    if S.cut is not None:
        for t in range(8):
            st_ = S.dma(y_d[t * 128:(t + 1) * 128, :], xbuf[t].ap, reads=[xbuf[t]], key=f"dbg{t}", nbytes=128 * 1024 * 4, name=f"dbg{t}")
            stores.append(st_.id)
    S.simulate(verbose=verbose)
    S.emit(final_waits=stores)
    S.close()
    return nc


_CACHE = {}


def _pack_small(inp, b):
    col = lambda v: np.ascontiguousarray(np.asarray(v, np.float32).reshape(-1, 128).T)
    sm = np.zeros((128, NS), np.float32)
    sm[:, C_C:C_C + 8] = col(inp["c"][b])
    sm[:, C_ADAB:C_ADAB + 48] = col(inp["ada_b"][0])
    sm[:, C_N1G:C_N1G + 8] = col(inp["norm1_g"][0])
    sm[:, C_N2G:C_N2G + 8] = col(inp["norm2_g"][0])
    lcw = np.asarray(inp["lru_conv_w"][0], np.float32)
    for c in range(4):
        for k in range(4):
            sm[:, C_LCW + c * 4 + k] = lcw[k, c * 128:(c + 1) * 128]
    sm[:, C_LCB:C_LCB + 4] = col(inp["lru_conv_b"][0])
    sm[:, C_GAB:C_GAB + 4] = col(inp["gate_a_b"][0])
    sm[:, C_GXB:C_GXB + 4] = col(inp["gate_x_b"][0])
    sm[:, C_AP:C_AP + 4] = col(inp["a_param"][0])
    sm[:, C_LOG:C_LOG + 4] = col(inp["lru_out_g"][0])
    sm[:, C_COG:C_COG + 4] = col(inp["conv_out_g"][0])
    scw = np.asarray(inp["short_conv_w"][0], np.float32)
    for c in range(4):
        for k in range(3):
            sm[:, C_SCW + c * 3 + k] = scw[k, c * 128:(c + 1) * 128]
    sm[:, C_FG:C_FG + 8] = col(inp["final_g"])
    return sm


def kernel(**inputs):
    inp = {k: np.asarray(v) for k, v in inputs.items()}
    if "nc" not in _CACHE:
        _CACHE["nc"] = build_program(verbose=False)
    nc = _CACHE["nc"]
    f32 = lambda a: np.ascontiguousarray(np.asarray(a, np.float32))
    shared = {
        "ada_w": f32(inp["ada_w"][0]),
        "w_in": f32(inp["w_in"][0]),
        "w_out": f32(inp["w_out"][0]),
        "w_mlp1": f32(inp["w_mlp1"][0]),
        "w_mlp2": f32(inp["w_mlp2"][0]),
        "gaw": f32(inp["gate_a_w"][0]).reshape(512, 64),
        "gxw": f32(inp["gate_x_w"][0]).reshape(512, 64),
    }
    in_maps = []
    for b in range(8):
        m = dict(shared)
        m["x"] = f32(inp["x"][b])
        m["small"] = _pack_small(inp, b)
        in_maps.append(m)
    res = run_bass_kernel_spmd(nc, in_maps, core_ids=list(range(8)))
    out = np.stack([np.asarray(res.results[b]["y"], np.float32) for b in range(8)], axis=0)
    return out
```
